# Optimizing a Trainium2 kernel written in Bass

```python
import math
import jax, jax.numpy as jnp
from jax import lax
import numpy as np

D_MODEL = 1024
BATCH = 32
SEQ = 2048
DEPTH = 2
DEC_BATCH = 8
DEC_SEQ = 2048
PAST_LEN = 128

N_META = 16
GRID_W = 64
BLOCK = 128
PAD = BLOCK - N_META
RET_HEADS = 4
RET_QK_DIM = 128
RET_V_DIM = 256
ATT_HEADS = 8
ATT_KV_HEADS = 2
ATT_HEAD_DIM = 128
D_FF = ((8 * D_MODEL + 3 * 256 - 1) // (3 * 256)) * 256
ROPE_BASE = 10000.0
NORM_EPS = 1e-6
RET_QK_W = RET_HEADS * RET_QK_DIM
RET_V_W = RET_HEADS * RET_V_DIM
ATT_Q_W = ATT_HEADS * ATT_HEAD_DIM
ATT_KV_W = ATT_KV_HEADS * ATT_HEAD_DIM
IN_SPLITS = (RET_QK_W, RET_QK_W, RET_V_W, RET_V_W, ATT_Q_W, ATT_KV_W, ATT_KV_W, D_MODEL, D_MODEL)
IN_WIDTH = RET_QK_W * 2 + RET_V_W * 2 + ATT_Q_W + ATT_KV_W * 2 + D_MODEL * 2

kernel_name = "hybrid_retention_gqa_encoder"


def _rms(x, gain=None):
    xf = x.astype(jnp.float32)
    y = xf * lax.rsqrt(jnp.mean(xf * xf, axis=-1, keepdims=True) + NORM_EPS)
    if gain is not None:
        y = y * gain.astype(jnp.float32)
    return y.astype(x.dtype)


def _rope(x, ang):
    half = x.shape[-1] // 2
    x1, x2 = x[..., :half], x[..., half:]
    c = jnp.cos(ang)[:, None, :].astype(x.dtype)
    s = jnp.sin(ang)[:, None, :].astype(x.dtype)
    return jnp.concatenate([x1 * c - x2 * s, x1 * s + x2 * c], axis=-1)


def _axial_rope(x, row_ang, col_ang):
    half = x.shape[-1] // 2
    return jnp.concatenate([_rope(x[..., :half], row_ang), _rope(x[..., half:], col_ang)], axis=-1)


def _inter_scan(qc, kc, vc, log_g, reverse):
    idx = jnp.arange(BLOCK, dtype=jnp.float32)
    if reverse:
        q_pow, k_pow = BLOCK - idx, idx
    else:
        q_pow, k_pow = idx + 1.0, BLOCK - 1.0 - idx
    q_dec = jnp.exp(log_g[:, None] * q_pow)[None, :, :, None]
    k_dec = jnp.exp(log_g[:, None] * k_pow)[None, :, :, None]
    chunk_dec = jnp.exp(log_g * BLOCK)[None, :, None, None]

    def step(state, inp):
        qi, ki, vi = inp
        out = jnp.einsum('bhid,bhdv->bhiv', qi * q_dec, state)
        state = state * chunk_dec + jnp.einsum('bhjd,bhjv->bhdv', ki * k_dec, vi)
        return state, out

    init = jnp.zeros((qc.shape[1], qc.shape[2], qc.shape[-1], vc.shape[-1]), jnp.float32)
    _, out = lax.scan(step, init, (qc, kc, vc), reverse=reverse)
    return out


def _retention_bidir(q, k, v, gamma):
    b, Lp, h, _ = q.shape
    nc = Lp // BLOCK

    def chunks(t):
        return t.reshape(b, nc, BLOCK, h, t.shape[-1]).transpose(1, 0, 3, 2, 4)

    qc, kc, vc = chunks(q), chunks(k), chunks(v)
    log_f, log_b = jnp.log(gamma[0]), jnp.log(gamma[1])
    idx = jnp.arange(BLOCK, dtype=jnp.float32)
    rel = idx[:, None] - idx[None, :]
    d_f = jnp.where(rel >= 0, jnp.exp(log_f[:, None, None] * jnp.maximum(rel, 0.0)), 0.0)
    d_b = jnp.where(rel < 0, jnp.exp(log_b[:, None, None] * jnp.maximum(-rel, 0.0)), 0.0)
    scores = jnp.einsum('cbhid,cbhjd->cbhij', qc, kc) * (d_f + d_b)
    out = jnp.einsum('cbhij,cbhjv->cbhiv', scores, vc)
    out = out + _inter_scan(qc, kc, vc, log_f, False) + _inter_scan(qc, kc, vc, log_b, True)
    return out.transpose(1, 0, 3, 2, 4).reshape(b, Lp, h, v.shape[-1])


def _block_attention(q, k, v):
    b, Lq, H, d = q.shape
    grp = H // ATT_KV_HEADS
    qb = q.reshape(b, Lq // BLOCK, BLOCK, ATT_KV_HEADS, grp, d).transpose(1, 0, 3, 4, 2, 5)
    scale = d ** -0.5

    def one_block(qi):
        s = jnp.einsum('bkgid,bjkd->bkgij', qi, k).astype(jnp.float32) * scale
        p = jax.nn.softmax(s, axis=-1).astype(v.dtype)
        return jnp.einsum('bkgij,bjkd->bkgid', p, v)

    o = lax.map(one_block, qb)
    return o.transpose(1, 0, 4, 2, 3, 5).reshape(b, Lq, H * d)


def _mixer(h, w_in, ret_decay, q_gain, k_gain, w_ret_o, w_att_o, w_out, ret_ang, row_ang, col_ang):
    b, L, _ = h.shape
    split_idx = tuple(int(i) for i in np.cumsum(IN_SPLITS)[:-1])
    proj = h @ w_in
    rq, rk, rv, rg, aq, ak, av, gr, ga = jnp.split(proj, split_idx, axis=-1)
    pad = ((0, 0), (PAD, 0), (0, 0), (0, 0))
    rq = _rope(rq.reshape(b, L, RET_HEADS, RET_QK_DIM), ret_ang).astype(jnp.float32)
    rk = _rope(rk.reshape(b, L, RET_HEADS, RET_QK_DIM), ret_ang).astype(jnp.float32) * (RET_QK_DIM ** -0.5)
    rv = rv.reshape(b, L, RET_HEADS, RET_V_DIM).astype(jnp.float32)
    gamma = 1.0 - jnp.exp2(-ret_decay.astype(jnp.float32))
    ret = _retention_bidir(jnp.pad(rq, pad), jnp.pad(rk, pad), jnp.pad(rv, pad), gamma)[:, PAD:]
    ret = _rms(ret).reshape(b, L, RET_V_W).astype(h.dtype)
    ret = (ret * jax.nn.silu(rg)) @ w_ret_o
    aq = _axial_rope(_rms(aq.reshape(b, L, ATT_HEADS, ATT_HEAD_DIM), q_gain), row_ang, col_ang)
    ak = _axial_rope(_rms(ak.reshape(b, L, ATT_KV_HEADS, ATT_HEAD_DIM), k_gain), row_ang, col_ang)
    av = av.reshape(b, L, ATT_KV_HEADS, ATT_HEAD_DIM)
    att = _block_attention(jnp.pad(aq, pad), ak, av)[:, PAD:]
    att = att @ w_att_o
    merged = jax.nn.sigmoid(gr) * ret + jax.nn.sigmoid(ga) * att
    return merged @ w_out


def _swiglu(h, w_ffn_in, w_ffn_out):
    a, u = jnp.split(h @ w_ffn_in, 2, axis=-1)
    return (jax.nn.silu(a) * u) @ w_ffn_out


def _trunk(x, meta_tokens, norm_mix, w_in, ret_decay, q_norm, k_norm, w_ret_o, w_att_o, w_out,
           norm_ffn, w_ffn_in, w_ffn_out, norm_final):
    b, S, _ = x.shape
    rows = S // GRID_W
    L = N_META + S
    h = jnp.concatenate([jnp.broadcast_to(meta_tokens.astype(x.dtype)[None], (b, N_META, D_MODEL)), x], axis=1)
    ret_inv = ROPE_BASE ** (-jnp.linspace(0.0, 1.0, RET_QK_DIM // 2, dtype=jnp.float32))
    ret_ang = jnp.arange(L, dtype=jnp.float32)[:, None] * ret_inv[None, :]
    zeros = jnp.zeros((N_META,), jnp.float32)
    row_ids = jnp.concatenate([zeros, jnp.repeat(jnp.arange(rows, dtype=jnp.float32), GRID_W)])
    col_ids = jnp.concatenate([zeros, jnp.tile(jnp.arange(GRID_W, dtype=jnp.float32), rows)])
    ax_half = ATT_HEAD_DIM // 2
    ax_inv = ROPE_BASE ** (-jnp.arange(ax_half // 2, dtype=jnp.float32) * 2.0 / ax_half)
    row_ang = row_ids[:, None] * ax_inv[None, :]
    col_ang = col_ids[:, None] * ax_inv[None, :]
    for l in range(DEPTH):
        h = h + _mixer(_rms(h, norm_mix[l]), w_in[l], ret_decay[l], q_norm[l], k_norm[l],
                       w_ret_o[l], w_att_o[l], w_out[l], ret_ang, row_ang, col_ang)
        h = h + _swiglu(_rms(h, norm_ffn[l]), w_ffn_in[l], w_ffn_out[l])
    return _rms(h, norm_final)[:, N_META:]


def setup_inputs(seed: int = 0) -> dict:
    key = jax.random.key(seed)
    ks = jax.random.split(key, 20)
    nrm = jax.random.normal
    base_decay = 5.0 + jnp.arange(RET_HEADS, dtype=jnp.float32)
    return {
        'x_prompt': nrm(ks[0], (BATCH, SEQ, D_MODEL), jnp.float32),
        'x_sample': nrm(ks[1], (DEC_BATCH, DEC_SEQ, D_MODEL), jnp.float32),
        'meta_tokens': nrm(ks[2], (N_META, D_MODEL), jnp.float32),
        'norm_mix': 1.0 + 0.02 * nrm(ks[3], (DEPTH, D_MODEL), jnp.float32),
        'w_in': nrm(ks[4], (DEPTH, D_MODEL, IN_WIDTH), jnp.float32) * D_MODEL ** -0.5,
        'ret_decay': base_decay[None, None, :] + 0.1 * nrm(ks[5], (DEPTH, 2, RET_HEADS), jnp.float32),
        'q_norm': 1.0 + 0.02 * nrm(ks[6], (DEPTH, ATT_HEAD_DIM), jnp.float32),
        'k_norm': 1.0 + 0.02 * nrm(ks[7], (DEPTH, ATT_HEAD_DIM), jnp.float32),
        'w_ret_o': nrm(ks[8], (DEPTH, RET_V_W, D_MODEL), jnp.float32) * RET_V_W ** -0.5,
        'w_att_o': nrm(ks[9], (DEPTH, ATT_Q_W, D_MODEL), jnp.float32) * ATT_Q_W ** -0.5,
        'w_out': nrm(ks[10], (DEPTH, D_MODEL, D_MODEL), jnp.float32) * D_MODEL ** -0.5,
        'norm_ffn': 1.0 + 0.02 * nrm(ks[11], (DEPTH, D_MODEL), jnp.float32),
        'w_ffn_in': nrm(ks[12], (DEPTH, D_MODEL, 2 * D_FF), jnp.float32) * D_MODEL ** -0.5,
        'w_ffn_out': nrm(ks[13], (DEPTH, D_FF, D_MODEL), jnp.float32) * D_FF ** -0.5,
        'norm_final': 1.0 + 0.02 * nrm(ks[14], (D_MODEL,), jnp.float32),
    }


def reference(x_prompt, x_sample, meta_tokens, norm_mix, w_in, ret_decay, q_norm, k_norm, w_ret_o,
              w_att_o, w_out, norm_ffn, w_ffn_in, w_ffn_out, norm_final):
    y_prompt = _trunk(x_prompt, meta_tokens, norm_mix, w_in, ret_decay, q_norm, k_norm, w_ret_o,
                      w_att_o, w_out, norm_ffn, w_ffn_in, w_ffn_out, norm_final)
    y_sample = _trunk(x_sample, meta_tokens, norm_mix, w_in, ret_decay, q_norm, k_norm, w_ret_o,
                      w_att_o, w_out, norm_ffn, w_ffn_in, w_ffn_out, norm_final)
    return (y_prompt, y_sample)
```

```python
import contextlib
import numpy as np
import concourse.bass as bass
import concourse.mybir as mybir

F32 = mybir.dt.float32
BF16 = mybir.dt.bfloat16
ALU = mybir.AluOpType
AF = mybir.ActivationFunctionType
AX = mybir.AxisListType

SEM_ROT = 8000


def _flat(rs):
    out = []
    for r in rs:
        if isinstance(r, (list, tuple)):
            out.extend(_flat(r))
        else:
            out.append(r)
    return out


class Res:
    __slots__ = ("name", "w", "r", "excl")

    def __init__(self, name):
        self.name = name
        self.excl = False
        self.w = None
        self.r = {}


class Prod:
    def __init__(self, sched, name, step):
        self.sched = sched
        self.name = name
        self.step = step
        self.epoch = 0
        self.count = 0
        self.sems = [sched.new_sem(f"{name}_0")]

    def _maybe_rotate(self):
        if self.count >= SEM_ROT:
            self.epoch += 1
            self.count = 0
            self.sems.append(self.sched.new_sem(f"{self.name}_{self.epoch}"))

    def bump(self):
        self._maybe_rotate()
        self.count += self.step
        return (self, self.epoch, self.count)

    def cur(self):
        self._maybe_rotate()
        return (self, self.epoch, self.count)


class Eng(Prod):
    def __init__(self, sched, name):
        super().__init__(sched, name, 1)
        self.ops = []
        self.waited = {}


class Sched:
    def __init__(self, nc):
        self.nc = nc
        self.stack = contextlib.ExitStack()
        self.nsem = 0
        self.pe = Eng(self, "pe")
        self.act = Eng(self, "act")
        self.dve = Eng(self, "dve")
        self.pool = Eng(self, "pool")
        self.sp = Eng(self, "sp")
        self.engs = [self.pe, self.act, self.dve, self.pool, self.sp]
        self.chans = {}
        self.rot = {}
        self.n_ops = 0

    def new_sem(self, name):
        self.nsem += 1
        return self.stack.enter_context(self.nc.semaphore(name))

    def sbuf(self, name, shape, dtype):
        return self.stack.enter_context(self.nc.sbuf_tensor(name, shape, dtype))

    def psum(self, name, shape, dtype):
        return self.stack.enter_context(self.nc.psum_tensor(name, shape, dtype))

    def _deps(self, eng, reads, writes):
        deps = {}
        strict = eng.name != "pe"

        def add(t):
            p, e, c = t
            k = (p, e)
            if deps.get(k, 0) < c:
                deps[k] = c

        for r in reads:
            if r.w is not None:
                add(r.w)
            if r.excl:
                for (p, e), c in r.r.items():
                    if p is not eng:
                        add((p, e, c))
        for w in writes:
            if w.w is not None and (w.w[0] is not eng or strict):
                add(w.w)
            for (p, e), c in w.r.items():
                if p is eng and not strict:
                    continue
                add((p, e, c))
        waits = []
        for (p, e), c in deps.items():
            k = (p.name, e)
            if eng.waited.get(k, 0) < c:
                eng.waited[k] = c
                waits.append((p.sems[e], c))
        return waits

    def _mark(self, tok, reads, writes):
        p, e, c = tok
        for r in reads:
            k = (p, e)
            if r.r.get(k, 0) < c:
                r.r[k] = c
        for w in writes:
            w.w = tok
            w.r = {}

    def op(self, eng, fn, reads=(), writes=(), signal=True):
        reads, writes = _flat(reads), _flat(writes)
        waits = self._deps(eng, reads, writes)
        if signal:
            tok = eng.bump()
            inc = (tok[0].sems[tok[1]], 1)
        else:
            p, e, c = eng.cur()
            tok = (p, e, c + 1)
            inc = None
        eng.ops.append((waits, fn, inc))
        self._mark(tok, reads, writes)
        self.n_ops += 1

    def chan(self, name):
        if name not in self.chans:
            self.chans[name] = Prod(self, "d" + name, 16)
        return self.chans[name]

    def rot_chan(self, group, n):
        i = self.rot.get(group, 0)
        self.rot[group] = i + 1
        return self.chan(f"{group}{i % n}")

    def dma(self, q, ch, out_ap, in_ap, reads=(), writes=(), **kw):
        reads, writes = _flat(reads), _flat(writes)
        waits = self._deps(q, reads, writes)
        k = (ch.name, ch.epoch)
        if ch.count > 0 and q.waited.get(k, 0) < ch.count:
            q.waited[k] = ch.count
            waits.append((ch.sems[ch.epoch], ch.count))
        tok = ch.bump()
        inc = (tok[0].sems[tok[1]], 16)
        q.ops.append((waits, lambda e: e.dma_start(out=out_ap, in_=in_ap, **kw), inc))
        self._mark(tok, reads, writes)
        self.n_ops += 1

    def barrier(self):
        prods = list(self.engs) + list(self.chans.values())
        for eng in self.engs:
            waits = []
            for p in prods:
                if p.count == 0 and p.epoch == 0:
                    continue
                k = (p.name, p.epoch)
                if eng.waited.get(k, 0) < p.count:
                    eng.waited[k] = p.count
                    waits.append((p.sems[p.epoch], p.count))
            if waits:
                eng.ops.append((waits, None, None))

    def wait_all(self, eng):
        prods = list(self.engs) + list(self.chans.values())
        waits = []
        for p in prods:
            if p is eng or (p.count == 0 and p.epoch == 0):
                continue
            waits.append((p.sems[p.epoch], p.count))
        eng.ops.append((waits, None, None))

    def emit(self):
        nc = self.nc
        with nc.Block() as block:
            def replay(eng):
                def body(e):
                    for waits, fn, inc in eng.ops:
                        for sem, val in waits:
                            e.wait_ge(sem, val)
                        if fn is None:
                            continue
                        ins = fn(e)
                        if inc is not None:
                            ins.then_inc(inc[0], inc[1])
                return body

            block.tensor(replay(self.pe))
            block.scalar(replay(self.act))
            block.vector(replay(self.dve))
            block.gpsimd(replay(self.pool))
            block.sync(replay(self.sp))

from concourse.bass_utils import run_bass_kernel_spmd

D = 1024
SEQ = 2048
NMETA = 16
L = SEQ + NMETA
NCH = 17
INW = 6656
DFF = 2816
LN2 = 0.6931471805599453
QK_SCALE = 128.0 ** -0.5
EPS = 1e-6
N_CORES = 8


def ntok(c):
    return 16 if c == 0 else 128


def col0(c):
    return 0 if c == 0 else 16 + (c - 1) * 128


TILES = [(0, 16, [0])] + [(16 + 512 * t, 512, [1 + 4 * t + i for i in range(4)]) for t in range(4)]
PASSES = [[0, 1], [2], [3], [4]]
FFN_PASSES = [[0, 1, 2], [3, 4]]


class Mem:
    def __init__(self, nc, base=16640, limit=229376):
        self.nc = nc
        self.top = base
        self.limit = limit
        self.n = 0

    def alloc(self, name, shape, dtype):
        esz = 4 if dtype == F32 else 2
        size = esz
        for s in shape[1:]:
            size *= s
        off = (self.top + 31) // 32 * 32
        self.top = off + size
        assert self.top <= self.limit, f"SBUF overflow at {name}: {self.top}"
        self.n += 1
        return self.nc.alloc_sbuf_tensor_at(f"{name}_{self.n}", list(shape), dtype, offset=off)


def build_program(nc, NS, DEPTH):
    dr = lambda name, shape, dt, kind: nc.dram_tensor(name, shape, dt, kind=kind).ap()
    x_d = dr("x", [NS, SEQ, D], F32, "ExternalInput")
    meta_d = dr("meta", [NMETA, D], F32, "ExternalInput")
    nmix_d = dr("norm_mix", [DEPTH, D], F32, "ExternalInput")
    win_d = dr("w_in", [DEPTH, D, INW], F32, "ExternalInput")
    dec_d = dr("ret_decay", [DEPTH, 8], F32, "ExternalInput")
    qn_d = dr("q_norm", [DEPTH, 128], F32, "ExternalInput")
    kn_d = dr("k_norm", [DEPTH, 128], F32, "ExternalInput")
    wro_d = dr("w_ret_o", [DEPTH, D, D], F32, "ExternalInput")
    wao_d = dr("w_att_o", [DEPTH, D, D], F32, "ExternalInput")
    wo_d = dr("w_out", [DEPTH, D, D], F32, "ExternalInput")
    nffn_d = dr("norm_ffn", [DEPTH, D], F32, "ExternalInput")
    wfi_d = dr("w_ffn_in", [DEPTH, D, 2 * DFF], F32, "ExternalInput")
    wfo_d = dr("w_ffn_out", [DEPTH, DFF, D], F32, "ExternalInput")
    nfin_d = dr("norm_final", [D], F32, "ExternalInput")
    tb_d = dr("rope_tb", [L, 4, 128], F32, "ExternalInput")
    y_d = dr("y", [NS, SEQ, D], F32, "ExternalOutput")
    bwin = dr("b_w_in", [DEPTH, D, INW], BF16, "Internal")
    bwro = dr("b_w_ret_o", [DEPTH, D, D], BF16, "Internal")
    bwao = dr("b_w_att_o", [DEPTH, D, D], BF16, "Internal")
    bwo = dr("b_w_out", [DEPTH, D, D], BF16, "Internal")
    bwfi = dr("b_w_ffn_in", [DEPTH, D, 2 * DFF], BF16, "Internal")
    bwfo = dr("b_w_ffn_out", [DEPTH, DFF, D], BF16, "Internal")

    S = Sched(nc)
    M = Mem(nc)
    with S.stack:
        ps = [S.psum(f"ps{i}", [128, 512], F32) for i in range(8)]
        psb = [p.bitcast(BF16) for p in ps]
        Rps = [Res(f"ps{i}") for i in range(8)]
        for r_ in Rps:
            r_.excl = True
        st = {"psi": 0, "wi": 0}

        def psn(nb=8):
            i = st["psi"] % nb
            st["psi"] += 1
            return i

        h = M.alloc("h", [128, NCH, D], F32)
        Rh = [Res(f"h{c}") for c in range(NCH)]
        hnT = M.alloc("hnT", [128, 8, L], BF16)
        RhA = [[Res(f"hnT{c}_{k}") for k in range(8)] for c in range(NCH)]
        NW = 2
        wring = [M.alloc(f"wr{i}", [128, 8, 512], BF16) for i in range(NW)]
        Rwh = [Res(f"wrh{i}") for i in range(2 * NW)]
        NTB = 4
        tbuf = [M.alloc(f"tb{i}", [128, 2, 128], F32) for i in range(NTB)]
        Rtb = [Res(f"tb{i}") for i in range(NTB)]
        ident = M.alloc("ident", [128, 128], BF16)
        ones = M.alloc("ones", [128, 128], BF16)
        epst = M.alloc("eps", [128, 1], F32)
        cst = M.alloc("cst", [128, 8, 128], F32)
        pcs = M.alloc("pcs", [128, 4], F32)
        DT = M.alloc("DT", [128, 4, 128], F32)
        qdf = M.alloc("qdf", [128, 4, 128], F32)
        qdb = M.alloc("qdb", [128, 4, 128], F32)
        dect = M.alloc("dect", [128, 8], F32)
        lg = M.alloc("lg", [128, 8], F32)
        gch = M.alloc("gch", [128, 8], F32)
        kdfs = M.alloc("kdfs", [128, 4], F32)
        kdbs = M.alloc("kdbs", [128, 4], F32)
        kdf0s = M.alloc("kdf0s", [128, 4], F32)
        gq = M.alloc("gq", [128, 128], F32)
        gk = M.alloc("gk", [128, 128], F32)
        ssn = M.alloc("ssn", [128, NCH], F32)
        rsn = M.alloc("rsn", [128, NCH], F32)
        Rssn = [Res(f"ssn{c}") for c in range(NCH)]
        Rrsn = [Res(f"rsn{c}") for c in range(NCH)]
        Rc = Res("consts")
        Rtab = Res("dectables")
        Rg = Res("gqk")
        PH0 = M.top

        def mm(out_ap, pairs, reads, writes):
            n = len(pairs)
            for i, (l_, r_) in enumerate(pairs):
                S.op(S.pe, lambda e, l_=l_, r_=r_, i=i: e.matmul(out_ap, lhsT=l_, rhs=r_, start=(i == 0), stop=(i == n - 1)),
                     reads=reads, writes=writes, signal=(i == n - 1))

        def tr(out_ap, in_ap, n, reads, writes, last):
            S.op(S.pe, lambda e: e.transpose(out=out_ap, in_=in_ap, identity=ident[:n, :n]),
                 reads=list(reads) + [Rc], writes=writes, signal=last)

        def act(out, in_, func, reads, writes, **kw):
            S.op(S.act, lambda e: e.activation(out=out, in_=in_, func=func, **kw), reads=reads, writes=writes)

        def tt(eng, out, in0, in1, op, reads, writes):
            S.op(eng, lambda e: e.tensor_tensor(out=out, in0=in0, in1=in1, op=op), reads=reads, writes=writes)

        def tsc(eng, out, in0, s1, op0, reads, writes, s2=None, op1=None):
            if op1 is None:
                S.op(eng, lambda e: e.tensor_scalar(out=out, in0=in0, scalar1=s1, scalar2=None, op0=op0), reads=reads, writes=writes)
            else:
                S.op(eng, lambda e: e.tensor_scalar(out=out, in0=in0, scalar1=s1, scalar2=s2, op0=op0, op1=op1), reads=reads, writes=writes)

        def stt(eng, out, in0, sc, in1, op0, op1, reads, writes):
            S.op(eng, lambda e: e.scalar_tensor_tensor(out=out, in0=in0, scalar=sc, in1=in1, op0=op0, op1=op1), reads=reads, writes=writes)

        def cp(eng, out, in_, reads, writes):
            S.op(eng, lambda e: e.tensor_copy(out=out, in_=in_), reads=reads, writes=writes)

        def hn_reads(chunks):
            return [RhA[c] for c in chunks]

        def wtile():
            i = st["wi"] % NW
            st["wi"] += 1
            return wring[i], [Rwh[2 * i], Rwh[2 * i + 1]]

        def htile():
            j = st.get("hi", 0) % (2 * NW)
            st["hi"] = st.get("hi", 0) + 1
            return wring[j // 2][:, :, (j % 2) * 256:(j % 2 + 1) * 256], [Rwh[j]]

        def wload(dst, Rdst, src, Rsrc, pieces):
            ch = S.rot_chan("w", 4)
            for (d0, s0, ncol) in pieces:
                S.dma(S.sp, ch, dst[:, :, d0:d0 + ncol], src[:, s0:s0 + ncol].rearrange("(k p) c -> p k c", p=128),
                      reads=Rsrc, writes=[Rdst])

        Rcast = {}

        def cast(name, dst, src, l, rows):
            rl = []
            for r0 in range(0, rows, 128):
                r = Res(f"{name}{l}_{r0}")
                S.dma(S.pool, S.rot_chan("c", 4), dst[l, r0:r0 + 128, :], src[l, r0:r0 + 128, :], writes=[r])
                rl.append(r)
            Rcast[(name, l)] = rl

        S.op(S.pool, lambda e: e.memset(ident[:], 1.0), writes=[Rc])
        S.op(S.pool, lambda e: e.affine_select(out=ident[:], in_=ident[:], pattern=[[-1, 128]], compare_op=ALU.is_equal,
                                               fill=0.0, base=0, channel_multiplier=1), reads=[Rc], writes=[Rc])
        S.op(S.pool, lambda e: e.memset(ones[:], 1.0), writes=[Rc])
        S.op(S.pool, lambda e: e.memset(epst[:], EPS), writes=[Rc])
        S.op(S.pool, lambda e: e.iota(cst[:, 6, :], pattern=[[1, 128]], base=0, channel_multiplier=-1,
                                      allow_small_or_imprecise_dtypes=True), writes=[Rc])
        tsc(S.dve, cst[:, 0, :], cst[:, 6, :], 0.0, ALU.max, [Rc], [Rc])
        tsc(S.dve, cst[:, 1, :], cst[:, 6, :], -1.0, ALU.mult, [Rc], [Rc], 0.0, ALU.max)
        tsc(S.dve, cst[:, 2, :], cst[:, 6, :], 0.0, ALU.is_ge, [Rc], [Rc], QK_SCALE, ALU.mult)
        tsc(S.dve, cst[:, 3, :], cst[:, 6, :], 0.0, ALU.is_lt, [Rc], [Rc], QK_SCALE, ALU.mult)
        S.op(S.pool, lambda e: e.iota(cst[:, 4, :], pattern=[[1, 128]], base=1, channel_multiplier=0,
                                      allow_small_or_imprecise_dtypes=True), reads=[Rc], writes=[Rc])
        S.op(S.pool, lambda e: e.iota(cst[:, 5, :], pattern=[[-1, 128]], base=128, channel_multiplier=0,
                                      allow_small_or_imprecise_dtypes=True), reads=[Rc], writes=[Rc])
        S.op(S.pool, lambda e: e.iota(pcs[:, 0:1], pattern=[[0, 1]], base=0, channel_multiplier=1,
                                      allow_small_or_imprecise_dtypes=True), reads=[Rc], writes=[Rc])
        S.op(S.pool, lambda e: e.iota(pcs[:, 1:2], pattern=[[0, 1]], base=127, channel_multiplier=-1,
                                      allow_small_or_imprecise_dtypes=True), reads=[Rc], writes=[Rc])
        S.op(S.pool, lambda e: e.iota(pcs[:, 2:3], pattern=[[0, 1]], base=15, channel_multiplier=-1,
                                      allow_small_or_imprecise_dtypes=True), reads=[Rc], writes=[Rc])

        for l in range(DEPTH):
            cast("win", bwin, win_d, l, D)
            cast("wro", bwro, wro_d, l, D)
            cast("wo", bwo, wo_d, l, D)
            cast("wao", bwao, wao_d, l, D)
            cast("wfi", bwfi, wfi_d, l, D)
            cast("wfo", bwfo, wfo_d, l, DFF)

        def load_tb(c, kind):
            i = st.get("tbi", 0) % NTB
            st["tbi"] = st.get("tbi", 0) + 1
            n = ntok(c)
            p0 = 0 if kind == "ret" else 2
            S.dma(S.sp, S.rot_chan("t", 3), tbuf[i][:n], tb_d[col0(c):col0(c) + n, p0:p0 + 2, :], writes=[Rtb[i]])
            return tbuf[i], Rtb[i]

        def rms_stats(src_ap, n, width, junk, Rjunk, ss_ap, rs_ap, Rss, Rrs, reads):
            act(junk, src_ap, AF.Square, reads, [Rjunk, Rss], accum_out=ss_ap)
            act(rs_ap, ss_ap, AF.Ln, [Rss, Rc], [Rrs], scale=1.0 / width, bias=epst[:n, :])
            act(rs_ap, rs_ap, AF.Exp, [Rrs], [Rrs], scale=-0.5)

        def norm_phase(gain_row):
            m0 = M.top
            hs = [M.alloc(f"hs{i}", [128, D], BF16) for i in range(2)]
            Rhs = [Res(f"hs{i}") for i in range(2)]
            sqj = M.alloc("sqj", [128, D], BF16)
            Rsqj = Res("sqj")
            gb = M.alloc("gbn", [128, D], F32)
            Rgb = Res("gbn")
            S.dma(S.sp, S.chan("g"), gb[:], gain_row.partition_broadcast(128), writes=[Rgb])
            pbank = {}

            def front(c):
                n, b = ntok(c), c % 2
                rms_stats(h[:n, c, :], n, D, sqj[:n, :], Rsqj, ssn[:n, c:c + 1], rsn[:n, c:c + 1], Rssn[c], Rrsn[c], [Rh[c]])
                stt(S.dve, hs[b][:n, :], h[:n, c, :], rsn[:n, c:c + 1], gb[:n, :], ALU.mult, ALU.mult, [Rh[c], Rrsn[c], Rgb], [Rhs[b]])
                pi = psn()
                pbank[c] = pi
                for k in range(8):
                    tr(psb[pi][:, k * 128:k * 128 + n], hs[b][:n, k * 128:(k + 1) * 128], n, [Rhs[b]], [Rps[pi]], k == 7)

            def back(c):
                n, cc, pi = ntok(c), col0(c), pbank[c]
                src = psb[pi][:, 0:1024].rearrange("p (k q) -> p k q", q=128)[:, :, :n]
                if c % 2 == 0:
                    act(hnT[:, :, cc:cc + n], src, AF.Copy, [Rps[pi]], [RhA[c]])
                else:
                    cp(S.dve, hnT[:, :, cc:cc + n], src, [Rps[pi]], [RhA[c]])

            front(0)
            if NCH > 1:
                front(1)
            for c in range(NCH):
                if c + 2 < NCH:
                    front(c + 2)
                back(c)
            S.barrier()
            M.top = m0

        def proj_tok(c, wt, Rw, ncols):
            n = ntok(c)
            cc = col0(c)
            pi = psn()
            mm(ps[pi][:n, :ncols], [(hnT[:, k, cc:cc + n], wt[:, k, :ncols]) for k in range(8)],
               hn_reads([c]) + [Rw], [Rps[pi]])
            return pi

        def rope(src, n, H, kind, tbt, Rt, t1, t2, Rt1, Rt2, out, Rout, src_reads):
            ci, si = 0, 1
            Cb = tbt[:n, ci:ci + 1, :].to_broadcast([n, H, 128])
            tt(S.dve, t1[:n, :H, :], src, Cb, ALU.mult, src_reads + [Rt], [Rt1])
            if kind == "ret":
                for hf in range(2):
                    o0, i0 = hf * 64, (1 - hf) * 64
                    Sb = tbt[:n, si:si + 1, o0:o0 + 64].to_broadcast([n, H, 64])
                    tt(S.dve, t2[:n, :H, o0:o0 + 64], src[:, :, i0:i0 + 64], Sb, ALU.mult, src_reads + [Rt], [Rt2[hf]])
            else:
                sv = src.rearrange("p h (a b c) -> p h a b c", a=2, b=2)
                tv = t2[:n, :H, :].rearrange("p h (a b c) -> p h a b c", a=2, b=2)
                Sv = tbt[:n, si, :].rearrange("p (a b c) -> p a b c", a=2, b=2)
                for b in range(2):
                    Sb = Sv[:, :, b, :].unsqueeze(1).to_broadcast([n, H, 2, 32])
                    tt(S.dve, tv[:, :, :, b, :], sv[:, :, :, 1 - b, :], Sb, ALU.mult, src_reads + [Rt], [Rt2[b]])
            tt(S.pool, out, t1[:n, :H, :], t2[:n, :H, :], ALU.add, [Rt1, Rt2], [Rout])

        def dec_tables(l):
            S.dma(S.sp, S.chan("g"), dect[:], dec_d[l].partition_broadcast(128), writes=[Rtab])
            S.dma(S.sp, S.chan("g"), gq[:], qn_d[l].partition_broadcast(128), writes=[Rg])
            S.dma(S.sp, S.chan("g"), gk[:], kn_d[l].partition_broadcast(128), writes=[Rg])
            T = [Rtab]
            act(lg[:], dect[:], AF.Exp, T, T, scale=-LN2)
            tsc(S.dve, lg[:], lg[:], -1.0, ALU.mult, T, T, 1.0, ALU.add)
            act(lg[:], lg[:], AF.Ln, T, T)
            act(gch[:], lg[:], AF.Exp, T, T, scale=128.0)
            for hd in range(4):
                act(cst[:, 6, :], cst[:, 0, :], AF.Exp, T + [Rc], T, scale=lg[:, hd:hd + 1])
                act(cst[:, 7, :], cst[:, 1, :], AF.Exp, T + [Rc], T, scale=lg[:, 4 + hd:5 + hd])
                tt(S.dve, cst[:, 6, :], cst[:, 6, :], cst[:, 2, :], ALU.mult, T + [Rc], T)
                tt(S.dve, cst[:, 7, :], cst[:, 7, :], cst[:, 3, :], ALU.mult, T + [Rc], T)
                tt(S.dve, DT[:, hd, :], cst[:, 6, :], cst[:, 7, :], ALU.add, T, T)
                act(qdf[:, hd, :], cst[:, 4, :], AF.Exp, T + [Rc], T, scale=lg[:, hd:hd + 1])
                act(qdb[:, hd, :], cst[:, 5, :], AF.Exp, T + [Rc], T, scale=lg[:, 4 + hd:5 + hd])
                act(kdfs[:, hd:hd + 1], pcs[:, 1:2], AF.Exp, T + [Rc], T, scale=lg[:, hd:hd + 1])
                act(kdbs[:, hd:hd + 1], pcs[:, 0:1], AF.Exp, T + [Rc], T, scale=lg[:, 4 + hd:5 + hd])
                act(kdf0s[:, hd:hd + 1], pcs[:, 2:3], AF.Exp, T + [Rc], T, scale=lg[:, hd:hd + 1])
            for t_ in (kdfs, kdbs, kdf0s):
                tsc(S.dve, t_[:], t_[:], QK_SCALE, ALU.mult, T, T)

        def retention(l, XT, RXT):
            m0 = M.top
            rqkT = M.alloc("rqkT", [128, 2, L], BF16)
            kdfh = M.alloc("kdfh", [128, NCH, 128], BF16)
            rvh = M.alloc("rvh", [128, NCH, 256], BF16)
            sball = M.alloc("sball", [128, NCH, 256], BF16)
            Rqk = [Res(f"rqk{c}") for c in range(NCH)]
            Rkd = [Res(f"kdf{c}") for c in range(NCH)]
            Rrv = [Res(f"rv{c}") for c in range(NCH)]
            Rsb = [Res(f"sb{c}") for c in range(NCH)]
            t1 = [M.alloc(f"rt1{i}", [128, 2, 128], F32) for i in range(2)]
            t2 = [M.alloc(f"rt2{i}", [128, 2, 128], F32) for i in range(2)]
            qkb = [M.alloc(f"qkb{i}", [128, 2, 128], BF16) for i in range(2)]
            kdb = [M.alloc(f"kdb{i}", [128, 128], BF16) for i in range(2)]
            Rt1 = [Res("t1") for _ in range(2)]
            Rt2 = [[Res("t2a"), Res("t2b")] for _ in range(2)]
            Rqkb = [Res("qkb") for _ in range(2)]
            Rkdb = [Res("kdb") for _ in range(2)]
            sb32 = M.alloc("sb32", [128, 256], F32)
            sf32 = M.alloc("sf32", [128, 256], F32)
            Rsb32, Rsf32 = Res("sb32"), Res("sf32")
            sfb = [M.alloc(f"sfb{i}", [128, 256], BF16) for i in range(2)]
            Rsfb = [Res("sfb") for _ in range(2)]
            AT = [M.alloc(f"AT{i}", [128, 128], BF16) for i in range(2)]
            QfT = [M.alloc(f"QfT{i}", [128, 128], BF16) for i in range(2)]
            QbT = [M.alloc(f"QbT{i}", [128, 128], BF16) for i in range(2)]
            RAT = [Res("AT") for _ in range(2)]
            RQf = [Res("Qf") for _ in range(2)]
            RQb = [Res("Qb") for _ in range(2)]
            sqo = M.alloc("sqo", [128, 256], BF16)
            Rsqo = Res("sqo")
            on = [M.alloc(f"on{i}", [128, 256], BF16) for i in range(2)]
            Ron = [Res("on") for _ in range(2)]
            sso = M.alloc("sso", [128, 2], F32)
            rso = M.alloc("rso", [128, 2], F32)
            Rsso = [Res("sso") for _ in range(2)]
            Rrso = [Res("rso") for _ in range(2)]
            Rw_in = Rcast[("win", l)]
            for hd in range(4):
                wt, Rw = wtile()
                wload(wt, Rw, bwin[l], Rw_in, [(0, hd * 128, 128), (128, 512 + hd * 128, 128), (256, 1024 + hd * 256, 256)])
                S.op(S.pool, lambda e: e.memset(sb32[:], 0.0), writes=[Rsb32])
                S.op(S.pool, lambda e: e.memset(sball[:, NCH - 1, :], 0.0), writes=[Rsb[NCH - 1]])
                pjb, ptb, tbs = {}, {}, {}

                def s1a(c):
                    tbs[c] = load_tb(c, "ret")
                    pjb[c] = proj_tok(c, wt, Rw, 512)

                def s1b(c):
                    n, b = ntok(c), c % 2
                    tbt, Rt = tbs[c]
                    pi = pjb[c]
                    src = ps[pi][:n, 0:256].rearrange("p (h d) -> p h d", h=2)
                    rope(src, n, 2, "ret", tbt, Rt, t1[b], t2[b], Rt1[b], Rt2[b], qkb[b][:n], Rqkb[b], [Rps[pi]])
                    cp(S.dve, rvh[:n, c, :], ps[pi][:n, 256:512], [Rps[pi]], [Rrv[c]])
                    ksc = kdf0s if c == 0 else kdfs
                    act(kdfh[:n, c, :], qkb[b][:n, 1, :], AF.Copy, [Rqkb[b], Rtab], [Rkd[c]], scale=ksc[:n, hd:hd + 1])
                    pt = psn()
                    ptb[c] = pt
                    for j in range(2):
                        tr(psb[pt][:, j * 128:j * 128 + n], qkb[b][:n, j, :], n, [Rqkb[b]], [Rps[pt]], j == 1)

                def s1c(c):
                    n, cc, b, pt = ntok(c), col0(c), c % 2, ptb[c]
                    if c >= 1:
                        tsc(S.dve, kdb[b][:n, :], qkb[b][:n, 1, :], kdbs[:n, hd:hd + 1], ALU.mult, [Rqkb[b], Rtab], [Rkdb[b]])
                    cp(S.dve, rqkT[:, :, cc:cc + n], psb[pt][:, 0:256].rearrange("p (j q) -> p j q", q=128)[:, :, :n],
                       [Rps[pt]], [Rqk[c]])
                    if c >= 1:
                        pd = psn()
                        mm(ps[pd][:, :256], [(kdb[b][:n, :], rvh[:n, c, :])], [Rkdb[b], Rrv[c]], [Rps[pd]])
                        stt(S.dve, sb32[:], sb32[:], gch[:, 4 + hd:5 + hd], ps[pd][:, :256], ALU.mult, ALU.add,
                            [Rsb32, Rps[pd], Rtab], [Rsb32])
                        act(sball[:, c - 1, :], sb32[:], AF.Copy, [Rsb32], [Rsb[c - 1]])

                order = list(range(NCH - 1, -1, -1))
                for i in range(NCH + 3):
                    if i < NCH:
                        s1a(order[i])
                    if 0 <= i - 2 < NCH:
                        s1b(order[i - 2])
                    if 0 <= i - 3 < NCH:
                        s1c(order[i - 3])
                S.op(S.pool, lambda e: e.memset(sf32[:], 0.0), writes=[Rsf32])

                def s3a(c):
                    n, cc, b = ntok(c), col0(c), c % 2
                    off = 112 if c == 0 else 0
                    p1 = psn()
                    mm(ps[p1][:n, :n], [(rqkT[:, 1, cc:cc + n], rqkT[:, 0, cc:cc + n])], [Rqk[c]], [Rps[p1]])
                    tt(S.dve, AT[b][:n, :n], ps[p1][:n, :n], DT[:n, hd, :n], ALU.mult, [Rps[p1], Rtab], [RAT[b]])
                    if c >= 1:
                        tt(S.pool, QfT[b][:, :n], rqkT[:, 0, cc:cc + n], qdf[:, hd, off:off + n], ALU.mult, [Rqk[c], Rtab], [RQf[b]])
                    if c < NCH - 1:
                        tt(S.pool, QbT[b][:, :n], rqkT[:, 0, cc:cc + n], qdb[:, hd, off:off + n], ALU.mult, [Rqk[c], Rtab], [RQb[b]])

                p3b = {}

                def s3b(c):
                    n, cc, b = ntok(c), col0(c), c % 2
                    pairs = [(AT[b][:n, :n], rvh[:n, c, :])]
                    rd = [RAT[b], Rrv[c]]
                    if c >= 1:
                        pairs.append((QfT[b][:, :n], sfb[c % 2][:, :]))
                        rd += [RQf[b], Rsfb[c % 2]]
                    if c < NCH - 1:
                        pairs.append((QbT[b][:, :n], sball[:, c, :]))
                        rd += [RQb[b], Rsb[c]]
                    p2 = psn()
                    mm(ps[p2][:n, :256], pairs, rd, [Rps[p2]])
                    if c < NCH - 1:
                        p4 = psn()
                        mm(ps[p4][:, :256], [(kdfh[:n, c, :], rvh[:n, c, :])], [Rkd[c], Rrv[c]], [Rps[p4]])
                        stt(S.dve, sf32[:], sf32[:], gch[:, hd:hd + 1], ps[p4][:, :256], ALU.mult, ALU.add,
                            [Rsf32, Rps[p4], Rtab], [Rsf32])
                        cp(S.dve, sfb[(c + 1) % 2][:, :], sf32[:], [Rsf32], [Rsfb[(c + 1) % 2]])
                    rms_stats(ps[p2][:n, :256], n, 256, sqo[:n, :], Rsqo, sso[:n, b:b + 1], rso[:n, b:b + 1], Rsso[b], Rrso[b], [Rps[p2]])
                    act(on[b][:n, :], ps[p2][:n, :256], AF.Copy, [Rps[p2], Rrso[b]], [Ron[b]], scale=rso[:n, b:b + 1])

                def s3c(c):
                    n, cc, b = ntok(c), col0(c), c % 2
                    p3 = psn()
                    for j in range(2):
                        tr(psb[p3][:, j * 128:j * 128 + n], on[b][:n, j * 128:(j + 1) * 128], n, [Ron[b]], [Rps[p3]], j == 1)
                    cp(S.dve, XT[:, 2 * hd:2 * hd + 2, cc:cc + n], psb[p3][:, 0:256].rearrange("p (j q) -> p j q", q=128)[:, :, :n],
                       [Rps[p3]], [RXT[c]])

                for i in range(NCH + 2):
                    if i < NCH:
                        s3a(i)
                    if 0 <= i - 1 < NCH:
                        s3b(i - 1)
                    if 0 <= i - 2 < NCH:
                        s3c(i - 2)
            S.barrier()
            M.top = m0

        def gate_rg(l, XT, RXT):
            m0 = M.top
            sg = [M.alloc(f"sgl{i}", [128, 512], F32) for i in range(2)]
            Rsg = [Res("sgl") for _ in range(2)]
            Rw_in = Rcast[("win", l)]
            it = 0
            for jb in range(2):
                wt, Rw = wtile()
                wload(wt, Rw, bwin[l], Rw_in, [(0, 2048 + jb * 512, 512)])
                for jj in range(4):
                    j = jb * 4 + jj
                    for (tc0, tn, chunks) in TILES:
                        pi = psn()
                        mm(ps[pi][:, :tn], [(wt[:, k, jj * 128:(jj + 1) * 128], hnT[:, k, tc0:tc0 + tn]) for k in range(8)],
                           hn_reads(chunks) + [Rw], [Rps[pi]])
                        b = it % 2
                        it += 1
                        act(sg[b][:, :tn], ps[pi][:, :tn], AF.Silu, [Rps[pi]], [Rsg[b]])
                        rx = [RXT[c] for c in chunks]
                        tt(S.pool, XT[:, j, tc0:tc0 + tn], XT[:, j, tc0:tc0 + tn], sg[b][:, :tn], ALU.mult, rx + [Rsg[b]], rx)
            S.barrier()
            M.top = m0

        def epilogue(l, srcT, Rsrc, bw, Rbw, gate_c0):
            m0 = M.top
            term = M.alloc("term", [128, 8, 528], BF16)
            Rterm = Res("term")
            sg = [M.alloc(f"sge{i}", [128, 512], F32) for i in range(2)]
            Rsg = [Res("sge") for _ in range(2)]
            Rw_in = Rcast[("win", l)]
            Rw_o = Rcast[("wo", l)]
            it = 0
            for pss in PASSES:
                tiles = [TILES[i] for i in pss]
                pc0 = tiles[0][0]
                for jb in range(4):
                    wa, Rwa = htile()
                    wload(wa, Rwa, bw[l], Rbw, [(0, jb * 256, 256)])
                    wg, Rwg = htile()
                    wload(wg, Rwg, bwin[l], Rw_in, [(0, gate_c0 + jb * 256, 256)])
                    for jj in range(2):
                        j = jb * 2 + jj
                        for (tc0, tn, chunks) in tiles:
                            pa = psn()
                            mm(ps[pa][:, :tn], [(wa[:, k, jj * 128:(jj + 1) * 128], srcT[:, k, tc0:tc0 + tn]) for k in range(8)],
                               [Rsrc[c] for c in chunks] + [Rwa], [Rps[pa]])
                            pg = psn()
                            mm(ps[pg][:, :tn], [(wg[:, k, jj * 128:(jj + 1) * 128], hnT[:, k, tc0:tc0 + tn]) for k in range(8)],
                               hn_reads(chunks) + [Rwg], [Rps[pg]])
                            b = it % 2
                            it += 1
                            act(sg[b][:, :tn], ps[pg][:, :tn], AF.Sigmoid, [Rps[pg]], [Rsg[b]])
                            tt(S.dve, term[:, j, tc0 - pc0:tc0 - pc0 + tn], ps[pa][:, :tn], sg[b][:, :tn], ALU.mult,
                               [Rps[pa], Rsg[b]], [Rterm])
                for cb in range(2):
                    wo2, Rwo2 = wtile()
                    wload(wo2, Rwo2, bwo[l], Rw_o, [(0, cb * 512, 512)])
                    for (tc0, tn, chunks) in tiles:
                        for c in chunks:
                            n, lc = ntok(c), col0(c) - pc0
                            po = psn()
                            mm(ps[po][:n, :512], [(term[:, k, lc:lc + n], wo2[:, k, :]) for k in range(8)], [Rterm, Rwo2], [Rps[po]])
                            tt(S.dve, h[:n, c, cb * 512:(cb + 1) * 512], h[:n, c, cb * 512:(cb + 1) * 512], ps[po][:n, :512], ALU.add,
                               [Rh[c], Rps[po]], [Rh[c]])
            S.barrier()
            M.top = m0

        def attention(l, AO, RAO):
            m0 = M.top
            akT = M.alloc("akT", [128, L], BF16)
            avg = M.alloc("avg", [128, NCH, 128], BF16)
            aqT = M.alloc("aqT", [128, 4, L], BF16)
            Rak = [Res(f"ak{c}") for c in range(NCH)]
            Rav = [Res(f"av{c}") for c in range(NCH)]
            Raq = [Res(f"aq{c}") for c in range(NCH)]
            t1 = [M.alloc("at1", [128, 4, 128], F32)]
            t2 = [M.alloc("at2", [128, 4, 128], F32)]
            qn = [M.alloc("aqn", [128, 4, 128], F32)]
            qb = [M.alloc(f"aqb{i}", [128, 4, 128], BF16) for i in range(2)]
            Rt1 = [Res("at1") for _ in range(2)]
            Rt2 = [[Res(f"at2{i}") for i in range(2)] for _ in range(2)]
            Rqn = [[Res(f"aqn{i}") for i in range(4)] for _ in range(2)]
            Rqb = [Res("aqb") for _ in range(2)]
            sqa = M.alloc("sqa", [128, 4, 128], BF16)
            Rsqa = [Res(f"sqa{i}") for i in range(4)]
            ssa = M.alloc("ssa", [128, 8], F32)
            rsa = M.alloc("rsa", [128, 8], F32)
            Rssa = [[Res(f"ssa{i}") for i in range(4)] for _ in range(2)]
            Rrsa = [Res("rsa") for _ in range(2)]
            NP = 4
            m_alias = M.top
            PT = [M.alloc(f"PT{i}", [128, 512], BF16) for i in range(NP)]
            RPT = [Res("PT") for _ in range(NP)]
            rec = [M.alloc(f"rec{i}", [128, 512], F32) for i in range(2)]
            Rrec = [Res("rec") for _ in range(2)]
            m_end = M.top
            M.top = m_alias
            t1.append(M.alloc("at1b", [128, 4, 128], F32))
            t2.append(M.alloc("at2b", [128, 4, 128], F32))
            qn.append(M.alloc("aqnb", [128, 4, 128], F32))
            M.top = max(M.top, m_end)
            Rw_in = Rcast[("win", l)]
            for g in range(2):
                wt, Rw = wtile()
                wload(wt, Rw, bwin[l], Rw_in, [(0, 4096 + g * 128, 128), (128, 4352 + g * 128, 128)])
                pjb, ptb, tbs = {}, {}, {}

                def ka(c):
                    tbs[c] = load_tb(c, "ax")
                    pjb[c] = proj_tok(c, wt, Rw, 256)

                def kb(c):
                    n, b = ntok(c), c % 2
                    tbt, Rt = tbs[c]
                    pi = pjb[c]
                    rms_stats(ps[pi][:n, 0:128], n, 128, sqa[:n, 0, :], Rsqa[0], ssa[:n, b:b + 1], rsa[:n, b:b + 1], Rssa[b][0], Rrsa[b], [Rps[pi]])
                    act(avg[:n, c, :], ps[pi][:n, 128:256], AF.Copy, [Rps[pi]], [Rav[c]])
                    stt(S.dve, qn[b][:n, 0, :], ps[pi][:n, 0:128], rsa[:n, b:b + 1], gk[:n, :], ALU.mult, ALU.mult,
                        [Rps[pi], Rrsa[b], Rg], [Rqn[b][0]])
                    rope(qn[b][:n, 0:1, :], n, 1, "ax", tbt, Rt, t1[b], t2[b], Rt1[b], Rt2[b], qb[b][:n, 0:1, :], Rqb[b], [Rqn[b][0]])
                    pt = psn()
                    ptb[c] = pt
                    tr(psb[pt][:, 0:n], qb[b][:n, 0, :], n, [Rqb[b]], [Rps[pt]], True)

                def kc_(c):
                    n, cc, pt = ntok(c), col0(c), ptb[c]
                    cp(S.dve, akT[:, cc:cc + n], psb[pt][:, 0:n], [Rps[pt]], [Rak[c]])

                for i in range(NCH + 3):
                    if i < NCH:
                        ka(i)
                    if 0 <= i - 2 < NCH:
                        kb(i - 2)
                    if 0 <= i - 3 < NCH:
                        kc_(i - 3)
                wt, Rw = wtile()
                wload(wt, Rw, bwin[l], Rw_in, [(0, 3072 + g * 512, 512)])
                pjb, ptb, tbs = {}, {}, {}

                def qa(c):
                    tbs[c] = load_tb(c, "ax")
                    pjb[c] = proj_tok(c, wt, Rw, 512)

                def qb_(c):
                    n, b = ntok(c), c % 2
                    tbt, Rt = tbs[c]
                    pi = pjb[c]
                    for hh in range(4):
                        act(sqa[:n, hh, :], ps[pi][:n, hh * 128:(hh + 1) * 128], AF.Square, [Rps[pi]], [Rsqa[hh], Rssa[b][hh]],
                            accum_out=ssa[:n, 4 * b + hh:4 * b + hh + 1])
                    act(rsa[:n, 4 * b:4 * b + 4], ssa[:n, 4 * b:4 * b + 4], AF.Ln, [Rssa[b], Rc], [Rrsa[b]], scale=1.0 / 128, bias=epst[:n, :])
                    act(rsa[:n, 4 * b:4 * b + 4], rsa[:n, 4 * b:4 * b + 4], AF.Exp, [Rrsa[b]], [Rrsa[b]], scale=-0.5)
                    for hh in range(4):
                        stt(S.dve, qn[b][:n, hh, :], ps[pi][:n, hh * 128:(hh + 1) * 128], rsa[:n, 4 * b + hh:4 * b + hh + 1], gq[:n, :],
                            ALU.mult, ALU.mult, [Rps[pi], Rrsa[b], Rg], [Rqn[b][hh]])
                    rope(qn[b][:n, :, :], n, 4, "ax", tbt, Rt, t1[b], t2[b], Rt1[b], Rt2[b], qb[b][:n, :, :], Rqb[b], [Rqn[b]])
                    pt = psn()
                    ptb[c] = pt
                    for hh in range(4):
                        tr(psb[pt][:, hh * 128:hh * 128 + n], qb[b][:n, hh, :], n, [Rqb[b]], [Rps[pt]], hh == 3)

                def qc_(c):
                    n, cc, pt = ntok(c), col0(c), ptb[c]
                    cp(S.dve, aqT[:, :, cc:cc + n], psb[pt][:, 0:512].rearrange("p (j q) -> p j q", q=128)[:, :, :n],
                       [Rps[pt]], [Raq[c]])

                for i in range(NCH + 3):
                    if i < NCH:
                        qa(i)
                    if 0 <= i - 2 < NCH:
                        qb_(i - 2)
                    if 0 <= i - 3 < NCH:
                        qc_(i - 3)
                S.barrier()
                items = []
                for ti, (tc0, tn, chunks) in enumerate(TILES):
                    for hh in range(4):
                        for kc in range(NCH):
                            items.append((ti, hh, kc))
                LA = 2
                slot = {}

                def sfront(i):
                    ti, hh, kc = items[i]
                    tc0, tn, chunks = TILES[ti]
                    nk, kc0 = ntok(kc), col0(kc)
                    pS = psn(4)
                    mm(ps[pS][:nk, :tn], [(akT[:, kc0:kc0 + nk], aqT[:, hh, tc0:tc0 + tn])], [Raq[c] for c in chunks] + [Rak[kc]], [Rps[pS]])
                    pb = i % NP
                    slot[i] = pb
                    act(PT[pb][:nk, :tn], ps[pS][:nk, :tn], AF.Exp, [Rps[pS]], [RPT[pb]], scale=QK_SCALE)

                def sback(i):
                    ti, hh, kc = items[i]
                    tc0, tn, chunks = TILES[ti]
                    nk = ntok(kc)
                    hidx = ti * 4 + hh
                    po, pm = 4 + hidx % 2, 6 + hidx % 2
                    pb = slot.pop(i)
                    S.op(S.pe, lambda e: e.matmul(ps[po][:, :tn], lhsT=avg[:nk, kc, :], rhs=PT[pb][:nk, :tn],
                                                  start=(kc == 0), stop=(kc == NCH - 1)),
                         reads=[Rav[kc], RPT[pb]], writes=[Rps[po]], signal=False)
                    S.op(S.pe, lambda e: e.matmul(ps[pm][:, :tn], lhsT=ones[:nk, :], rhs=PT[pb][:nk, :tn],
                                                  start=(kc == 0), stop=(kc == NCH - 1)),
                         reads=[Rc, RPT[pb]], writes=[Rps[pm]], signal=True)
                    if kc == NCH - 1:
                        head = g * 4 + hh
                        rb = hidx % 2
                        S.op(S.dve, lambda e: e.reciprocal(out=rec[rb][:, :tn], in_=ps[pm][:, :tn]),
                             reads=[Rps[pm]], writes=[Rrec[rb]])
                        tt(S.dve, AO[:, head, tc0:tc0 + tn], ps[po][:, :tn], rec[rb][:, :tn], ALU.mult,
                           [Rps[po], Rrec[rb]], [RAO[c] for c in chunks])

                for i in range(len(items) + LA):
                    if i < len(items):
                        sfront(i)
                    if i >= LA:
                        sback(i - LA)
                if g == 0:
                    S.barrier()
            S.barrier()
            M.top = m0

        def ffn(l):
            m0 = M.top
            gT = M.alloc("gT", [128, 22, 1040], BF16)
            RgT = Res("gT")
            wfo = M.alloc("wfo", [128, 22, 512], BF16)
            Rwfo = Res("wfo")
            sa = [M.alloc(f"sa{i}", [128, 512], F32) for i in range(2)]
            Rsa = [Res("sa") for _ in range(2)]
            Rw_fi = Rcast[("wfi", l)]
            Rw_fo = Rcast[("wfo", l)]
            it = 0
            def load_wfo(cb):
                S.dma(S.sp, S.rot_chan("w", 4), wfo[:], bwfo[l][:, cb * 512:(cb + 1) * 512].rearrange("(k p) c -> p k c", p=128),
                      reads=Rw_fo, writes=[Rwfo])

            for pss in FFN_PASSES:
                tiles = [TILES[i] for i in pss]
                pc0 = tiles[0][0]
                load_wfo(0)
                for jb in range(11):
                    wa, Rwa = htile()
                    wload(wa, Rwa, bwfi[l], Rw_fi, [(0, jb * 256, 256)])
                    wu, Rwu = htile()
                    wload(wu, Rwu, bwfi[l], Rw_fi, [(0, DFF + jb * 256, 256)])
                    for jj in range(2):
                        j = jb * 2 + jj
                        for (tc0, tn, chunks) in tiles:
                            pa = psn()
                            mm(ps[pa][:, :tn], [(wa[:, k, jj * 128:(jj + 1) * 128], hnT[:, k, tc0:tc0 + tn]) for k in range(8)],
                               hn_reads(chunks) + [Rwa], [Rps[pa]])
                            pu = psn()
                            mm(ps[pu][:, :tn], [(wu[:, k, jj * 128:(jj + 1) * 128], hnT[:, k, tc0:tc0 + tn]) for k in range(8)],
                               hn_reads(chunks) + [Rwu], [Rps[pu]])
                            b = it % 2
                            it += 1
                            act(sa[b][:, :tn], ps[pa][:, :tn], AF.Silu, [Rps[pa]], [Rsa[b]])
                            tt(S.dve, gT[:, j, tc0 - pc0:tc0 - pc0 + tn], ps[pu][:, :tn], sa[b][:, :tn], ALU.mult,
                               [Rps[pu], Rsa[b]], [RgT])
                for cb in range(2):
                    if cb == 1:
                        load_wfo(1)
                    for (tc0, tn, chunks) in tiles:
                        for c in chunks:
                            n, lc = ntok(c), col0(c) - pc0
                            po = psn()
                            mm(ps[po][:n, :512], [(gT[:, k, lc:lc + n], wfo[:, k, :]) for k in range(22)], [RgT, Rwfo], [Rps[po]])
                            tt(S.dve, h[:n, c, cb * 512:(cb + 1) * 512], h[:n, c, cb * 512:(cb + 1) * 512], ps[po][:n, :512], ALU.add,
                               [Rh[c], Rps[po]], [Rh[c]])
            S.barrier()
            M.top = m0

        def final_phase(s):
            m0 = M.top
            gbf = M.alloc("gbf", [128, D], F32)
            Rgbf = Res("gbf")
            yo = [M.alloc(f"yo{i}", [128, D], F32) for i in range(2)]
            Ryo = [Res("yo") for _ in range(2)]
            sqj = M.alloc("sqjf", [128, D], BF16)
            Rsqj = Res("sqjf")
            S.dma(S.sp, S.chan("g"), gbf[:], nfin_d.partition_broadcast(128), writes=[Rgbf])
            for c in range(1, NCH):
                b = c % 2
                rms_stats(h[:, c, :], 128, D, sqj[:, :], Rsqj, ssn[:, c:c + 1], rsn[:, c:c + 1], Rssn[c], Rrsn[c], [Rh[c]])
                stt(S.dve, yo[b][:], h[:, c, :], rsn[:, c:c + 1], gbf[:], ALU.mult, ALU.mult, [Rh[c], Rrsn[c], Rgbf], [Ryo[b]])
                S.dma(S.sp, S.rot_chan("y", 2), y_d[s, (c - 1) * 128:c * 128, :], yo[b][:], reads=[Ryo[b]])
            S.barrier()
            M.top = m0

        RXT = [Res(f"XT{c}") for c in range(NCH)]
        for s in range(NS):
            S.dma(S.sp, S.rot_chan("x", 2), h[:NMETA, 0, :], meta_d, writes=[Rh[0]])
            for t in range(4):
                S.dma(S.sp, S.rot_chan("x", 2), h[:, 1 + 4 * t:5 + 4 * t, :],
                      x_d[s, 512 * t:512 * (t + 1), :].rearrange("(c p) d -> p c d", p=128),
                      writes=[Rh[1 + 4 * t + i] for i in range(4)])
            for l in range(DEPTH):
                norm_phase(nmix_d[l])
                dec_tables(l)
                m_mix = M.top
                XT = M.alloc("XT", [128, 8, L], BF16)
                retention(l, XT, RXT)
                gate_rg(l, XT, RXT)
                epilogue(l, XT, RXT, bwro, Rcast[("wro", l)], 4608)
                attention(l, XT, RXT)
                epilogue(l, XT, RXT, bwao, Rcast[("wao", l)], 5632)
                M.top = m_mix
                norm_phase(nffn_d[l])
                ffn(l)
            final_phase(s)
        S.wait_all(S.sp)
        S.emit()
    return nc


_TB_CACHE = {}


def rope_tables():
    if "tb" in _TB_CACHE:
        return _TB_CACHE["tb"]
    f32 = np.float32
    ret_inv = (f32(10000.0) ** (-np.linspace(0.0, 1.0, 64, dtype=f32))).astype(f32)
    ret_ang = (np.arange(L, dtype=f32)[:, None] * ret_inv[None, :]).astype(f32)
    rows = SEQ // 64
    zeros = np.zeros((NMETA,), f32)
    row_ids = np.concatenate([zeros, np.repeat(np.arange(rows, dtype=f32), 64)])
    col_ids = np.concatenate([zeros, np.tile(np.arange(64, dtype=f32), rows)])
    ax_inv = (f32(10000.0) ** (-np.arange(32, dtype=f32) * f32(2.0) / f32(64))).astype(f32)
    row_ang = (row_ids[:, None] * ax_inv[None, :]).astype(f32)
    col_ang = (col_ids[:, None] * ax_inv[None, :]).astype(f32)
    tb = np.zeros((L, 4, 128), f32)
    c, s = np.cos(ret_ang), np.sin(ret_ang)
    tb[:, 0, :64], tb[:, 0, 64:] = c, c
    tb[:, 1, :64], tb[:, 1, 64:] = -s, s
    cr, sr, cc, sc = np.cos(row_ang), np.sin(row_ang), np.cos(col_ang), np.sin(col_ang)
    tb[:, 2, 0:32], tb[:, 2, 32:64], tb[:, 2, 64:96], tb[:, 2, 96:128] = cr, cr, cc, cc
    tb[:, 3, 0:32], tb[:, 3, 32:64], tb[:, 3, 64:96], tb[:, 3, 96:128] = -sr, sr, -sc, sc
    _TB_CACHE["tb"] = tb
    return tb


def make_in_maps(xs_per_core, meta_tokens, norm_mix, w_in, ret_decay, q_norm, k_norm, w_ret_o, w_att_o, w_out,
                 norm_ffn, w_ffn_in, w_ffn_out, norm_final):
    c = lambda a: np.ascontiguousarray(np.asarray(a, dtype=np.float32))
    depth = np.asarray(w_in).shape[0]
    shared = {
        "meta": c(meta_tokens), "norm_mix": c(norm_mix), "w_in": c(w_in),
        "ret_decay": c(np.asarray(ret_decay).reshape(depth, 8)), "q_norm": c(q_norm), "k_norm": c(k_norm),
        "w_ret_o": c(w_ret_o), "w_att_o": c(w_att_o), "w_out": c(w_out), "norm_ffn": c(norm_ffn),
        "w_ffn_in": c(w_ffn_in), "w_ffn_out": c(w_ffn_out), "norm_final": c(norm_final), "rope_tb": rope_tables(),
    }
    return [dict(shared, x=c(xc)) for xc in xs_per_core]


def kernel(x_prompt, x_sample, meta_tokens, norm_mix, w_in, ret_decay, q_norm, k_norm, w_ret_o, w_att_o, w_out,
           norm_ffn, w_ffn_in, w_ffn_out, norm_final):
    xp = np.asarray(x_prompt, dtype=np.float32)
    xs = np.asarray(x_sample, dtype=np.float32)
    allx = np.concatenate([xp, xs], axis=0)
    nseq = allx.shape[0]
    per = nseq // N_CORES
    depth = np.asarray(w_in).shape[0]
    nc = bass.Bass("TRN2", target_bir_lowering=False)
    build_program(nc, per, depth)
    in_maps = make_in_maps([allx[i * per:(i + 1) * per] for i in range(N_CORES)], meta_tokens, norm_mix, w_in, ret_decay,
                           q_norm, k_norm, w_ret_o, w_att_o, w_out, norm_ffn, w_ffn_in, w_ffn_out, norm_final)
    res = run_bass_kernel_spmd(nc, in_maps, core_ids=list(range(N_CORES)))
    y = np.concatenate([np.asarray(r["y"], dtype=np.float32) for r in res.results], axis=0)
    return (y[:xp.shape[0]], y[xp.shape[0]:])
```

```python
import contextlib
import numpy as np
import concourse.bass as bass
import concourse.mybir as mybir

F32 = mybir.dt.float32
BF16 = mybir.dt.bfloat16
ALU = mybir.AluOpType
AF = mybir.ActivationFunctionType
AX = mybir.AxisListType

STRICT_SAME_ENGINE = False
SEM_ROT = 8000


def _flat(rs):
    out = []
    for r in rs:
        if isinstance(r, (list, tuple)):
            out.extend(_flat(r))
        else:
            out.append(r)
    return out


class Res:
    __slots__ = ("name", "w", "r", "excl")

    def __init__(self, name):
        self.name = name
        self.excl = False
        self.w = None
        self.r = {}


class Prod:
    def __init__(self, sched, name, step):
        self.sched = sched
        self.name = name
        self.step = step
        self.epoch = 0
        self.count = 0
        self.sems = [sched.new_sem(f"{name}_0")]

    def _maybe_rotate(self):
        if self.count >= SEM_ROT:
            self.epoch += 1
            self.count = 0
            self.sems.append(self.sched.new_sem(f"{self.name}_{self.epoch}"))

    def bump(self):
        self._maybe_rotate()
        self.count += self.step
        return (self, self.epoch, self.count)

    def cur(self):
        self._maybe_rotate()
        return (self, self.epoch, self.count)


class Eng(Prod):
    def __init__(self, sched, name):
        super().__init__(sched, name, 1)
        self.ops = []
        self.waited = {}


class Sched:
    def __init__(self, nc):
        self.nc = nc
        self.stack = contextlib.ExitStack()
        self.nsem = 0
        self.pe = Eng(self, "pe")
        self.act = Eng(self, "act")
        self.dve = Eng(self, "dve")
        self.pool = Eng(self, "pool")
        self.sp = Eng(self, "sp")
        self.engs = [self.pe, self.act, self.dve, self.pool, self.sp]
        self.chans = {}
        self.rot = {}
        self.n_ops = 0

    def new_sem(self, name):
        self.nsem += 1
        return self.stack.enter_context(self.nc.semaphore(name))

    def sbuf(self, name, shape, dtype):
        return self.stack.enter_context(self.nc.sbuf_tensor(name, shape, dtype))

    def psum(self, name, shape, dtype):
        return self.stack.enter_context(self.nc.psum_tensor(name, shape, dtype))

    def _deps(self, eng, reads, writes):
        deps = {}
        strict = eng.name == "pool" or (STRICT_SAME_ENGINE and eng.name != "pe")

        def add(t):
            p, e, c = t
            k = (p, e)
            if deps.get(k, 0) < c:
                deps[k] = c

        for r in reads:
            if r.w is not None:
                add(r.w)
            if r.excl:
                for (p, e), c in r.r.items():
                    if p is not eng:
                        add((p, e, c))
        for w in writes:
            if w.w is not None and (w.w[0] is not eng or strict):
                add(w.w)
            for (p, e), c in w.r.items():
                if p is eng and not strict:
                    continue
                add((p, e, c))
        waits = []
        for (p, e), c in deps.items():
            k = (p.name, e)
            if eng.waited.get(k, 0) < c:
                eng.waited[k] = c
                waits.append((p.sems[e], c))
        return waits

    def _mark(self, tok, reads, writes):
        p, e, c = tok
        for r in reads:
            k = (p, e)
            if r.r.get(k, 0) < c:
                r.r[k] = c
        for w in writes:
            w.w = tok
            w.r = {}

    def op(self, eng, fn, reads=(), writes=(), signal=True):
        reads, writes = _flat(reads), _flat(writes)
        waits = self._deps(eng, reads, writes)
        if signal:
            tok = eng.bump()
            inc = (tok[0].sems[tok[1]], 1)
        else:
            p, e, c = eng.cur()
            tok = (p, e, c + 1)
            inc = None
        eng.ops.append((waits, fn, inc))
        self._mark(tok, reads, writes)
        self.n_ops += 1

    def chan(self, name):
        if name not in self.chans:
            self.chans[name] = Prod(self, "d" + name, 16)
        return self.chans[name]

    def rot_chan(self, group, n):
        i = self.rot.get(group, 0)
        self.rot[group] = i + 1
        return self.chan(f"{group}{i % n}")

    def dma(self, q, ch, out_ap, in_ap, reads=(), writes=(), **kw):
        reads, writes = _flat(reads), _flat(writes)
        waits = self._deps(q, reads, writes)
        k = (ch.name, ch.epoch)
        if ch.count > 0 and q.waited.get(k, 0) < ch.count:
            q.waited[k] = ch.count
            waits.append((ch.sems[ch.epoch], ch.count))
        tok = ch.bump()
        inc = (tok[0].sems[tok[1]], 16)
        q.ops.append((waits, lambda e: e.dma_start(out=out_ap, in_=in_ap, **kw), inc))
        self._mark(tok, reads, writes)
        self.n_ops += 1

    def barrier(self):
        prods = list(self.engs) + list(self.chans.values())
        for eng in self.engs:
            waits = []
            for p in prods:
                if p.count == 0 and p.epoch == 0:
                    continue
                k = (p.name, p.epoch)
                if eng.waited.get(k, 0) < p.count:
                    eng.waited[k] = p.count
                    waits.append((p.sems[p.epoch], p.count))
            if waits:
                eng.ops.append((waits, None, None))

    def wait_all(self, eng):
        prods = list(self.engs) + list(self.chans.values())
        waits = []
        for p in prods:
            if p is eng or (p.count == 0 and p.epoch == 0):
                continue
            waits.append((p.sems[p.epoch], p.count))
        eng.ops.append((waits, None, None))

    def emit(self):
        nc = self.nc
        with nc.Block() as block:
            def replay(eng):
                def body(e):
                    for waits, fn, inc in eng.ops:
                        for sem, val in waits:
                            e.wait_ge(sem, val)
                        if fn is None:
                            continue
                        ins = fn(e)
                        if inc is not None:
                            ins.then_inc(inc[0], inc[1])
                return body

            block.tensor(replay(self.pe))
            block.scalar(replay(self.act))
            block.vector(replay(self.dve))
            block.gpsimd(replay(self.pool))
            block.sync(replay(self.sp))

from concourse.bass_utils import run_bass_kernel_spmd

D = 1024
SEQ = 2048
NMETA = 16
L = SEQ + NMETA
NCH = 17
INW = 6656
DFF = 2816
LN2 = 0.6931471805599453
QK_SCALE = 128.0 ** -0.5
EPS = 1e-6
N_CORES = 8


def ntok(c):
    return 16 if c == 0 else 128


def col0(c):
    return 0 if c == 0 else 16 + (c - 1) * 128


TILES = [(0, 16, [0])] + [(16 + 512 * t, 512, [1 + 4 * t + i for i in range(4)]) for t in range(4)]
PASSES = [[0, 1], [2], [3], [4]]
FFN_PASSES = [[0, 1, 2], [3, 4]]


class Mem:
    def __init__(self, nc, base=16640, limit=229376):
        self.nc = nc
        self.top = base
        self.limit = limit
        self.n = 0

    def alloc(self, name, shape, dtype):
        esz = 4 if dtype == F32 else 2
        size = esz
        for s in shape[1:]:
            size *= s
        off = (self.top + 31) // 32 * 32
        self.top = off + size
        assert self.top <= self.limit, f"SBUF overflow at {name}: {self.top}"
        self.n += 1
        return self.nc.alloc_sbuf_tensor_at(f"{name}_{self.n}", list(shape), dtype, offset=off)


def build_program(nc, NS, DEPTH):
    dr = lambda name, shape, dt, kind: nc.dram_tensor(name, shape, dt, kind=kind).ap()
    x_d = dr("x", [NS, SEQ, D], F32, "ExternalInput")
    meta_d = dr("meta", [NMETA, D], F32, "ExternalInput")
    nmix_d = dr("norm_mix", [DEPTH, D], F32, "ExternalInput")
    win_d = dr("w_in", [DEPTH, D, INW], F32, "ExternalInput")
    dec_d = dr("ret_decay", [DEPTH, 8], F32, "ExternalInput")
    qn_d = dr("q_norm", [DEPTH, 128], F32, "ExternalInput")
    kn_d = dr("k_norm", [DEPTH, 128], F32, "ExternalInput")
    wro_d = dr("w_ret_o", [DEPTH, D, D], F32, "ExternalInput")
    wao_d = dr("w_att_o", [DEPTH, D, D], F32, "ExternalInput")
    wo_d = dr("w_out", [DEPTH, D, D], F32, "ExternalInput")
    nffn_d = dr("norm_ffn", [DEPTH, D], F32, "ExternalInput")
    wfi_d = dr("w_ffn_in", [DEPTH, D, 2 * DFF], F32, "ExternalInput")
    wfo_d = dr("w_ffn_out", [DEPTH, DFF, D], F32, "ExternalInput")
    nfin_d = dr("norm_final", [D], F32, "ExternalInput")
    tb_d = dr("rope_tb", [L, 4, 128], F32, "ExternalInput")
    y_d = dr("y", [NS, SEQ, D], F32, "ExternalOutput")
    bwin = dr("b_w_in", [DEPTH, D, INW], BF16, "Internal")
    bwro = dr("b_w_ret_o", [DEPTH, D, D], BF16, "Internal")
    bwao = dr("b_w_att_o", [DEPTH, D, D], BF16, "Internal")
    bwo = dr("b_w_out", [DEPTH, D, D], BF16, "Internal")
    bwfi = dr("b_w_ffn_in", [DEPTH, D, 2 * DFF], BF16, "Internal")
    bwfo = dr("b_w_ffn_out", [DEPTH, DFF, D], BF16, "Internal")

    S = Sched(nc)
    M = Mem(nc)
    with S.stack:
        ps = [S.psum(f"ps{i}", [128, 512], F32) for i in range(8)]
        psb = [p.bitcast(BF16) for p in ps]
        Rps = [Res(f"ps{i}") for i in range(8)]
        for r_ in Rps:
            r_.excl = True
        st = {"psi": 0, "wi": 0}

        def psn(nb=8):
            i = st["psi"] % nb
            st["psi"] += 1
            return i

        h = M.alloc("h", [128, NCH, D], F32)
        Rh = [Res(f"h{c}") for c in range(NCH)]
        hnT = M.alloc("hnT", [128, 8, L], BF16)
        RhA = [[Res(f"hnT{c}_{k}") for k in range(8)] for c in range(NCH)]
        NW = 2
        wring = [M.alloc(f"wr{i}", [128, 8, 512], BF16) for i in range(NW)]
        Rwh = [Res(f"wrh{i}") for i in range(2 * NW)]
        NTB = 4
        tbuf = [M.alloc(f"tb{i}", [128, 2, 128], F32) for i in range(NTB)]
        Rtb = [Res(f"tb{i}") for i in range(NTB)]
        ident = M.alloc("ident", [128, 128], BF16)
        ones = M.alloc("ones", [128, 128], BF16)
        epst = M.alloc("eps", [128, 1], F32)
        cst = M.alloc("cst", [128, 8, 128], F32)
        pcs = M.alloc("pcs", [128, 4], F32)
        DT = M.alloc("DT", [128, 4, 128], F32)
        qdf = M.alloc("qdf", [128, 4, 128], F32)
        qdb = M.alloc("qdb", [128, 4, 128], F32)
        dect = M.alloc("dect", [128, 8], F32)
        lg = M.alloc("lg", [128, 8], F32)
        gch = M.alloc("gch", [128, 8], F32)
        kdfs = M.alloc("kdfs", [128, 4], F32)
        kdbs = M.alloc("kdbs", [128, 4], F32)
        kdf0s = M.alloc("kdf0s", [128, 4], F32)
        gq = M.alloc("gq", [128, 128], F32)
        gk = M.alloc("gk", [128, 128], F32)
        ssn = M.alloc("ssn", [128, NCH], F32)
        rsn = M.alloc("rsn", [128, NCH], F32)
        Rssn = [Res(f"ssn{c}") for c in range(NCH)]
        Rrsn = [Res(f"rsn{c}") for c in range(NCH)]
        Rc = Res("consts")
        Rtab = Res("dectables")
        Rg = Res("gqk")
        PH0 = M.top

        def mm(out_ap, pairs, reads, writes):
            n = len(pairs)
            for i, (l_, r_) in enumerate(pairs):
                S.op(S.pe, lambda e, l_=l_, r_=r_, i=i: e.matmul(out_ap, lhsT=l_, rhs=r_, start=(i == 0), stop=(i == n - 1)),
                     reads=reads, writes=writes, signal=(i == n - 1))

        def tr(out_ap, in_ap, n, reads, writes, last):
            S.op(S.pe, lambda e: e.transpose(out=out_ap, in_=in_ap, identity=ident[:n, :n]),
                 reads=list(reads) + [Rc], writes=writes, signal=last)

        def act(out, in_, func, reads, writes, **kw):
            S.op(S.act, lambda e: e.activation(out=out, in_=in_, func=func, **kw), reads=reads, writes=writes)

        def tt(eng, out, in0, in1, op, reads, writes):
            S.op(eng, lambda e: e.tensor_tensor(out=out, in0=in0, in1=in1, op=op), reads=reads, writes=writes)

        def tsc(eng, out, in0, s1, op0, reads, writes, s2=None, op1=None):
            if op1 is None:
                S.op(eng, lambda e: e.tensor_scalar(out=out, in0=in0, scalar1=s1, scalar2=None, op0=op0), reads=reads, writes=writes)
            else:
                S.op(eng, lambda e: e.tensor_scalar(out=out, in0=in0, scalar1=s1, scalar2=s2, op0=op0, op1=op1), reads=reads, writes=writes)

        def stt(eng, out, in0, sc, in1, op0, op1, reads, writes):
            S.op(eng, lambda e: e.scalar_tensor_tensor(out=out, in0=in0, scalar=sc, in1=in1, op0=op0, op1=op1), reads=reads, writes=writes)

        def cp(eng, out, in_, reads, writes):
            S.op(eng, lambda e: e.tensor_copy(out=out, in_=in_), reads=reads, writes=writes)

        def hn_reads(chunks):
            return [RhA[c] for c in chunks]

        def wtile():
            i = st["wi"] % NW
            st["wi"] += 1
            return wring[i], [Rwh[2 * i], Rwh[2 * i + 1]]

        def htile():
            j = st.get("hi", 0) % (2 * NW)
            st["hi"] = st.get("hi", 0) + 1
            return wring[j // 2][:, :, (j % 2) * 256:(j % 2 + 1) * 256], [Rwh[j]]

        def wload(dst, Rdst, src, Rsrc, pieces):
            ch = S.rot_chan("w", 4)
            for (d0, s0, ncol) in pieces:
                S.dma(S.sp, ch, dst[:, :, d0:d0 + ncol], src[:, s0:s0 + ncol].rearrange("(k p) c -> p k c", p=128),
                      reads=Rsrc, writes=[Rdst])

        Rcast = {}

        def cast(name, dst, src, l, rows):
            rl = []
            for r0 in range(0, rows, 128):
                r = Res(f"{name}{l}_{r0}")
                S.dma(S.pool, S.rot_chan("c", 4), dst[l, r0:r0 + 128, :], src[l, r0:r0 + 128, :], writes=[r])
                rl.append(r)
            Rcast[(name, l)] = rl

        S.op(S.pool, lambda e: e.memset(ident[:], 1.0), writes=[Rc])
        S.op(S.pool, lambda e: e.affine_select(out=ident[:], in_=ident[:], pattern=[[-1, 128]], compare_op=ALU.is_equal,
                                               fill=0.0, base=0, channel_multiplier=1), reads=[Rc], writes=[Rc])
        S.op(S.pool, lambda e: e.memset(ones[:], 1.0), writes=[Rc])
        S.op(S.pool, lambda e: e.memset(epst[:], EPS), writes=[Rc])
        S.op(S.pool, lambda e: e.iota(cst[:, 6, :], pattern=[[1, 128]], base=0, channel_multiplier=-1,
                                      allow_small_or_imprecise_dtypes=True), writes=[Rc])
        tsc(S.dve, cst[:, 0, :], cst[:, 6, :], 0.0, ALU.max, [Rc], [Rc])
        tsc(S.dve, cst[:, 1, :], cst[:, 6, :], -1.0, ALU.mult, [Rc], [Rc], 0.0, ALU.max)
        tsc(S.dve, cst[:, 2, :], cst[:, 6, :], 0.0, ALU.is_ge, [Rc], [Rc], QK_SCALE, ALU.mult)
        tsc(S.dve, cst[:, 3, :], cst[:, 6, :], 0.0, ALU.is_lt, [Rc], [Rc], QK_SCALE, ALU.mult)
        S.op(S.pool, lambda e: e.iota(cst[:, 4, :], pattern=[[1, 128]], base=1, channel_multiplier=0,
                                      allow_small_or_imprecise_dtypes=True), reads=[Rc], writes=[Rc])
        S.op(S.pool, lambda e: e.iota(cst[:, 5, :], pattern=[[-1, 128]], base=128, channel_multiplier=0,
                                      allow_small_or_imprecise_dtypes=True), reads=[Rc], writes=[Rc])
        S.op(S.pool, lambda e: e.iota(pcs[:, 0:1], pattern=[[0, 1]], base=0, channel_multiplier=1,
                                      allow_small_or_imprecise_dtypes=True), reads=[Rc], writes=[Rc])
        S.op(S.pool, lambda e: e.iota(pcs[:, 1:2], pattern=[[0, 1]], base=127, channel_multiplier=-1,
                                      allow_small_or_imprecise_dtypes=True), reads=[Rc], writes=[Rc])
        S.op(S.pool, lambda e: e.iota(pcs[:, 2:3], pattern=[[0, 1]], base=15, channel_multiplier=-1,
                                      allow_small_or_imprecise_dtypes=True), reads=[Rc], writes=[Rc])

        for l in range(DEPTH):
            cast("win", bwin, win_d, l, D)
            cast("wro", bwro, wro_d, l, D)
            cast("wo", bwo, wo_d, l, D)
            cast("wao", bwao, wao_d, l, D)
            cast("wfi", bwfi, wfi_d, l, D)
            cast("wfo", bwfo, wfo_d, l, DFF)

        def load_tb(c, kind):
            i = st.get("tbi", 0) % NTB
            st["tbi"] = st.get("tbi", 0) + 1
            n = ntok(c)
            p0 = 0 if kind == "ret" else 2
            S.dma(S.sp, S.rot_chan("t", 3), tbuf[i][:n], tb_d[col0(c):col0(c) + n, p0:p0 + 2, :], writes=[Rtb[i]])
            return tbuf[i], Rtb[i]

        def rms_stats(src_ap, n, width, junk, Rjunk, ss_ap, rs_ap, Rss, Rrs, reads):
            act(junk, src_ap, AF.Square, reads, [Rjunk, Rss], accum_out=ss_ap)
            act(rs_ap, ss_ap, AF.Ln, [Rss, Rc], [Rrs], scale=1.0 / width, bias=epst[:n, :])
            act(rs_ap, rs_ap, AF.Exp, [Rrs], [Rrs], scale=-0.5)

        def norm_phase(gain_row):
            m0 = M.top
            hs = [M.alloc(f"hs{i}", [128, D], BF16) for i in range(2)]
            Rhs = [Res(f"hs{i}") for i in range(2)]
            sqj = M.alloc("sqj", [128, D], BF16)
            Rsqj = Res("sqj")
            gb = M.alloc("gbn", [128, D], F32)
            Rgb = Res("gbn")
            S.dma(S.sp, S.chan("g"), gb[:], gain_row.partition_broadcast(128), writes=[Rgb])
            pbank = {}

            def front(c):
                n, b = ntok(c), c % 2
                rms_stats(h[:n, c, :], n, D, sqj[:n, :], Rsqj, ssn[:n, c:c + 1], rsn[:n, c:c + 1], Rssn[c], Rrsn[c], [Rh[c]])
                stt(S.dve, hs[b][:n, :], h[:n, c, :], rsn[:n, c:c + 1], gb[:n, :], ALU.mult, ALU.mult, [Rh[c], Rrsn[c], Rgb], [Rhs[b]])
                pi = psn()
                pbank[c] = pi
                for k in range(8):
                    tr(psb[pi][:, k * 128:k * 128 + n], hs[b][:n, k * 128:(k + 1) * 128], n, [Rhs[b]], [Rps[pi]], k == 7)

            def back(c):
                n, cc, pi = ntok(c), col0(c), pbank[c]
                src = psb[pi][:, 0:1024].rearrange("p (k q) -> p k q", q=128)[:, :, :n]
                if c % 2 == 0:
                    act(hnT[:, :, cc:cc + n], src, AF.Copy, [Rps[pi]], [RhA[c]])
                else:
                    cp(S.dve, hnT[:, :, cc:cc + n], src, [Rps[pi]], [RhA[c]])

            front(0)
            if NCH > 1:
                front(1)
            for c in range(NCH):
                if c + 2 < NCH:
                    front(c + 2)
                back(c)
            S.barrier()
            M.top = m0

        def proj_tok(c, wt, Rw, ncols):
            n = ntok(c)
            cc = col0(c)
            pi = psn()
            mm(ps[pi][:n, :ncols], [(hnT[:, k, cc:cc + n], wt[:, k, :ncols]) for k in range(8)],
               hn_reads([c]) + [Rw], [Rps[pi]])
            return pi

        def rope(src, n, H, kind, tbt, Rt, t1, t2, Rt1, Rt2, out, Rout, src_reads):
            ci, si = 0, 1
            Cb = tbt[:n, ci:ci + 1, :].to_broadcast([n, H, 128])
            tt(S.dve, t1[:n, :H, :], src, Cb, ALU.mult, src_reads + [Rt], [Rt1])
            if kind == "ret":
                for hf in range(2):
                    o0, i0 = hf * 64, (1 - hf) * 64
                    Sb = tbt[:n, si:si + 1, o0:o0 + 64].to_broadcast([n, H, 64])
                    tt(S.dve, t2[:n, :H, o0:o0 + 64], src[:, :, i0:i0 + 64], Sb, ALU.mult, src_reads + [Rt], [Rt2[hf]])
            else:
                sv = src.rearrange("p h (a b c) -> p h a b c", a=2, b=2)
                tv = t2[:n, :H, :].rearrange("p h (a b c) -> p h a b c", a=2, b=2)
                Sv = tbt[:n, si, :].rearrange("p (a b c) -> p a b c", a=2, b=2)
                for b in range(2):
                    Sb = Sv[:, :, b, :].unsqueeze(1).to_broadcast([n, H, 2, 32])
                    tt(S.dve, tv[:, :, :, b, :], sv[:, :, :, 1 - b, :], Sb, ALU.mult, src_reads + [Rt], [Rt2[b]])
            tt(S.pool, out, t1[:n, :H, :], t2[:n, :H, :], ALU.add, [Rt1, Rt2], [Rout])

        def dec_tables(l):
            S.dma(S.sp, S.chan("g"), dect[:], dec_d[l].partition_broadcast(128), writes=[Rtab])
            S.dma(S.sp, S.chan("g"), gq[:], qn_d[l].partition_broadcast(128), writes=[Rg])
            S.dma(S.sp, S.chan("g"), gk[:], kn_d[l].partition_broadcast(128), writes=[Rg])
            T = [Rtab]
            act(lg[:], dect[:], AF.Exp, T, T, scale=-LN2)
            tsc(S.dve, lg[:], lg[:], -1.0, ALU.mult, T, T, 1.0, ALU.add)
            act(lg[:], lg[:], AF.Ln, T, T)
            act(gch[:], lg[:], AF.Exp, T, T, scale=128.0)
            for hd in range(4):
                act(cst[:, 6, :], cst[:, 0, :], AF.Exp, T + [Rc], T, scale=lg[:, hd:hd + 1])
                act(cst[:, 7, :], cst[:, 1, :], AF.Exp, T + [Rc], T, scale=lg[:, 4 + hd:5 + hd])
                tt(S.dve, cst[:, 6, :], cst[:, 6, :], cst[:, 2, :], ALU.mult, T + [Rc], T)
                tt(S.dve, cst[:, 7, :], cst[:, 7, :], cst[:, 3, :], ALU.mult, T + [Rc], T)
                tt(S.dve, DT[:, hd, :], cst[:, 6, :], cst[:, 7, :], ALU.add, T, T)
                act(qdf[:, hd, :], cst[:, 4, :], AF.Exp, T + [Rc], T, scale=lg[:, hd:hd + 1])
                act(qdb[:, hd, :], cst[:, 5, :], AF.Exp, T + [Rc], T, scale=lg[:, 4 + hd:5 + hd])
                act(kdfs[:, hd:hd + 1], pcs[:, 1:2], AF.Exp, T + [Rc], T, scale=lg[:, hd:hd + 1])
                act(kdbs[:, hd:hd + 1], pcs[:, 0:1], AF.Exp, T + [Rc], T, scale=lg[:, 4 + hd:5 + hd])
                act(kdf0s[:, hd:hd + 1], pcs[:, 2:3], AF.Exp, T + [Rc], T, scale=lg[:, hd:hd + 1])
            for t_ in (kdfs, kdbs, kdf0s):
                tsc(S.dve, t_[:], t_[:], QK_SCALE, ALU.mult, T, T)

        def retention(l, XT, RXT):
            m0 = M.top
            rqkT = M.alloc("rqkT", [128, 2, L], BF16)
            kdfh = M.alloc("kdfh", [128, NCH, 128], BF16)
            rvh = M.alloc("rvh", [128, NCH, 256], BF16)
            sball = M.alloc("sball", [128, NCH, 256], BF16)
            Rqk = [Res(f"rqk{c}") for c in range(NCH)]
            Rkd = [Res(f"kdf{c}") for c in range(NCH)]
            Rrv = [Res(f"rv{c}") for c in range(NCH)]
            Rsb = [Res(f"sb{c}") for c in range(NCH)]
            t1 = [M.alloc(f"rt1{i}", [128, 2, 128], F32) for i in range(2)]
            t2 = [M.alloc(f"rt2{i}", [128, 2, 128], F32) for i in range(2)]
            qkb = [M.alloc(f"qkb{i}", [128, 2, 128], BF16) for i in range(2)]
            kdb = [M.alloc(f"kdb{i}", [128, 128], BF16) for i in range(2)]
            Rt1 = [Res("t1") for _ in range(2)]
            Rt2 = [[Res("t2a"), Res("t2b")] for _ in range(2)]
            Rqkb = [Res("qkb") for _ in range(2)]
            Rkdb = [Res("kdb") for _ in range(2)]
            sb32 = M.alloc("sb32", [128, 256], F32)
            sf32 = M.alloc("sf32", [128, 256], F32)
            Rsb32, Rsf32 = Res("sb32"), Res("sf32")
            sfb = [M.alloc(f"sfb{i}", [128, 256], BF16) for i in range(2)]
            Rsfb = [Res("sfb") for _ in range(2)]
            AT = [M.alloc(f"AT{i}", [128, 128], BF16) for i in range(2)]
            QfT = [M.alloc(f"QfT{i}", [128, 128], BF16) for i in range(2)]
            QbT = [M.alloc(f"QbT{i}", [128, 128], BF16) for i in range(2)]
            RAT = [Res("AT") for _ in range(2)]
            RQf = [Res("Qf") for _ in range(2)]
            RQb = [Res("Qb") for _ in range(2)]
            sqo = M.alloc("sqo", [128, 256], BF16)
            Rsqo = Res("sqo")
            on = [M.alloc(f"on{i}", [128, 256], BF16) for i in range(2)]
            Ron = [Res("on") for _ in range(2)]
            sso = M.alloc("sso", [128, 2], F32)
            rso = M.alloc("rso", [128, 2], F32)
            Rsso = [Res("sso") for _ in range(2)]
            Rrso = [Res("rso") for _ in range(2)]
            Rw_in = Rcast[("win", l)]
            for hd in range(4):
                wt, Rw = wtile()
                wload(wt, Rw, bwin[l], Rw_in, [(0, hd * 128, 128), (128, 512 + hd * 128, 128), (256, 1024 + hd * 256, 256)])
                S.op(S.pool, lambda e: e.memset(sb32[:], 0.0), writes=[Rsb32])
                S.op(S.pool, lambda e: e.memset(sball[:, NCH - 1, :], 0.0), writes=[Rsb[NCH - 1]])
                pjb, ptb, tbs = {}, {}, {}

                def s1a(c):
                    tbs[c] = load_tb(c, "ret")
                    pjb[c] = proj_tok(c, wt, Rw, 512)

                def s1b(c):
                    n, b = ntok(c), c % 2
                    tbt, Rt = tbs[c]
                    pi = pjb[c]
                    src = ps[pi][:n, 0:256].rearrange("p (h d) -> p h d", h=2)
                    rope(src, n, 2, "ret", tbt, Rt, t1[b], t2[b], Rt1[b], Rt2[b], qkb[b][:n], Rqkb[b], [Rps[pi]])
                    cp(S.dve, rvh[:n, c, :], ps[pi][:n, 256:512], [Rps[pi]], [Rrv[c]])
                    ksc = kdf0s if c == 0 else kdfs
                    act(kdfh[:n, c, :], qkb[b][:n, 1, :], AF.Copy, [Rqkb[b], Rtab], [Rkd[c]], scale=ksc[:n, hd:hd + 1])
                    pt = psn()
                    ptb[c] = pt
                    for j in range(2):
                        tr(psb[pt][:, j * 128:j * 128 + n], qkb[b][:n, j, :], n, [Rqkb[b]], [Rps[pt]], j == 1)

                def s1c(c):
                    n, cc, b, pt = ntok(c), col0(c), c % 2, ptb[c]
                    if c >= 1:
                        tsc(S.dve, kdb[b][:n, :], qkb[b][:n, 1, :], kdbs[:n, hd:hd + 1], ALU.mult, [Rqkb[b], Rtab], [Rkdb[b]])
                    cp(S.dve, rqkT[:, :, cc:cc + n], psb[pt][:, 0:256].rearrange("p (j q) -> p j q", q=128)[:, :, :n],
                       [Rps[pt]], [Rqk[c]])
                    if c >= 1:
                        pd = psn()
                        mm(ps[pd][:, :256], [(kdb[b][:n, :], rvh[:n, c, :])], [Rkdb[b], Rrv[c]], [Rps[pd]])
                        stt(S.dve, sb32[:], sb32[:], gch[:, 4 + hd:5 + hd], ps[pd][:, :256], ALU.mult, ALU.add,
                            [Rsb32, Rps[pd], Rtab], [Rsb32])
                        act(sball[:, c - 1, :], sb32[:], AF.Copy, [Rsb32], [Rsb[c - 1]])

                order = list(range(NCH - 1, -1, -1))
                for i in range(NCH + 3):
                    if i < NCH:
                        s1a(order[i])
                    if 0 <= i - 2 < NCH:
                        s1b(order[i - 2])
                    if 0 <= i - 3 < NCH:
                        s1c(order[i - 3])
                S.op(S.pool, lambda e: e.memset(sf32[:], 0.0), writes=[Rsf32])

                def s3a(c):
                    n, cc, b = ntok(c), col0(c), c % 2
                    off = 112 if c == 0 else 0
                    p1 = psn()
                    mm(ps[p1][:n, :n], [(rqkT[:, 1, cc:cc + n], rqkT[:, 0, cc:cc + n])], [Rqk[c]], [Rps[p1]])
                    tt(S.dve, AT[b][:n, :n], ps[p1][:n, :n], DT[:n, hd, :n], ALU.mult, [Rps[p1], Rtab], [RAT[b]])
                    if c >= 1:
                        tt(S.pool, QfT[b][:, :n], rqkT[:, 0, cc:cc + n], qdf[:, hd, off:off + n], ALU.mult, [Rqk[c], Rtab], [RQf[b]])
                    if c < NCH - 1:
                        tt(S.pool, QbT[b][:, :n], rqkT[:, 0, cc:cc + n], qdb[:, hd, off:off + n], ALU.mult, [Rqk[c], Rtab], [RQb[b]])

                p3b = {}

                def s3b(c):
                    n, cc, b = ntok(c), col0(c), c % 2
                    pairs = [(AT[b][:n, :n], rvh[:n, c, :])]
                    rd = [RAT[b], Rrv[c]]
                    if c >= 1:
                        pairs.append((QfT[b][:, :n], sfb[c % 2][:, :]))
                        rd += [RQf[b], Rsfb[c % 2]]
                    if c < NCH - 1:
                        pairs.append((QbT[b][:, :n], sball[:, c, :]))
                        rd += [RQb[b], Rsb[c]]
                    p2 = psn()
                    mm(ps[p2][:n, :256], pairs, rd, [Rps[p2]])
                    if c < NCH - 1:
                        p4 = psn()
                        mm(ps[p4][:, :256], [(kdfh[:n, c, :], rvh[:n, c, :])], [Rkd[c], Rrv[c]], [Rps[p4]])
                        stt(S.dve, sf32[:], sf32[:], gch[:, hd:hd + 1], ps[p4][:, :256], ALU.mult, ALU.add,
                            [Rsf32, Rps[p4], Rtab], [Rsf32])
                        cp(S.dve, sfb[(c + 1) % 2][:, :], sf32[:], [Rsf32], [Rsfb[(c + 1) % 2]])
                    rms_stats(ps[p2][:n, :256], n, 256, sqo[:n, :], Rsqo, sso[:n, b:b + 1], rso[:n, b:b + 1], Rsso[b], Rrso[b], [Rps[p2]])
                    act(on[b][:n, :], ps[p2][:n, :256], AF.Copy, [Rps[p2], Rrso[b]], [Ron[b]], scale=rso[:n, b:b + 1])

                def s3c(c):
                    n, cc, b = ntok(c), col0(c), c % 2
                    p3 = psn()
                    for j in range(2):
                        tr(psb[p3][:, j * 128:j * 128 + n], on[b][:n, j * 128:(j + 1) * 128], n, [Ron[b]], [Rps[p3]], j == 1)
                    cp(S.dve, XT[:, 2 * hd:2 * hd + 2, cc:cc + n], psb[p3][:, 0:256].rearrange("p (j q) -> p j q", q=128)[:, :, :n],
                       [Rps[p3]], [RXT[c]])

                for i in range(NCH + 2):
                    if i < NCH:
                        s3a(i)
                    if 0 <= i - 1 < NCH:
                        s3b(i - 1)
                    if 0 <= i - 2 < NCH:
                        s3c(i - 2)
            S.barrier()
            M.top = m0

        def gate_rg(l, XT, RXT):
            m0 = M.top
            sg = [M.alloc(f"sgl{i}", [128, 512], F32) for i in range(2)]
            Rsg = [Res("sgl") for _ in range(2)]
            Rw_in = Rcast[("win", l)]
            it = 0
            for jb in range(2):
                wt, Rw = wtile()
                wload(wt, Rw, bwin[l], Rw_in, [(0, 2048 + jb * 512, 512)])
                for jj in range(4):
                    j = jb * 4 + jj
                    for (tc0, tn, chunks) in TILES:
                        pi = psn()
                        mm(ps[pi][:, :tn], [(wt[:, k, jj * 128:(jj + 1) * 128], hnT[:, k, tc0:tc0 + tn]) for k in range(8)],
                           hn_reads(chunks) + [Rw], [Rps[pi]])
                        b = it % 2
                        it += 1
                        act(sg[b][:, :tn], ps[pi][:, :tn], AF.Silu, [Rps[pi]], [Rsg[b]])
                        rx = [RXT[c] for c in chunks]
                        tt(S.pool, XT[:, j, tc0:tc0 + tn], XT[:, j, tc0:tc0 + tn], sg[b][:, :tn], ALU.mult, rx + [Rsg[b]], rx)
            S.barrier()
            M.top = m0

        def epilogue(l, srcT, Rsrc, bw, Rbw, gate_c0):
            m0 = M.top
            term = M.alloc("term", [128, 8, 528], BF16)
            Rterm = Res("term")
            sg = [M.alloc(f"sge{i}", [128, 512], F32) for i in range(2)]
            Rsg = [Res("sge") for _ in range(2)]
            Rw_in = Rcast[("win", l)]
            Rw_o = Rcast[("wo", l)]
            it = 0
            for pss in PASSES:
                tiles = [TILES[i] for i in pss]
                pc0 = tiles[0][0]
                for jb in range(4):
                    wa, Rwa = htile()
                    wload(wa, Rwa, bw[l], Rbw, [(0, jb * 256, 256)])
                    wg, Rwg = htile()
                    wload(wg, Rwg, bwin[l], Rw_in, [(0, gate_c0 + jb * 256, 256)])
                    for jj in range(2):
                        j = jb * 2 + jj
                        for (tc0, tn, chunks) in tiles:
                            pa = psn()
                            mm(ps[pa][:, :tn], [(wa[:, k, jj * 128:(jj + 1) * 128], srcT[:, k, tc0:tc0 + tn]) for k in range(8)],
                               [Rsrc[c] for c in chunks] + [Rwa], [Rps[pa]])
                            pg = psn()
                            mm(ps[pg][:, :tn], [(wg[:, k, jj * 128:(jj + 1) * 128], hnT[:, k, tc0:tc0 + tn]) for k in range(8)],
                               hn_reads(chunks) + [Rwg], [Rps[pg]])
                            b = it % 2
                            it += 1
                            act(sg[b][:, :tn], ps[pg][:, :tn], AF.Sigmoid, [Rps[pg]], [Rsg[b]])
                            tt(S.dve, term[:, j, tc0 - pc0:tc0 - pc0 + tn], ps[pa][:, :tn], sg[b][:, :tn], ALU.mult,
                               [Rps[pa], Rsg[b]], [Rterm])
                for cb in range(2):
                    wo2, Rwo2 = wtile()
                    wload(wo2, Rwo2, bwo[l], Rw_o, [(0, cb * 512, 512)])
                    for (tc0, tn, chunks) in tiles:
                        for c in chunks:
                            n, lc = ntok(c), col0(c) - pc0
                            po = psn()
                            mm(ps[po][:n, :512], [(term[:, k, lc:lc + n], wo2[:, k, :]) for k in range(8)], [Rterm, Rwo2], [Rps[po]])
                            tt(S.dve, h[:n, c, cb * 512:(cb + 1) * 512], h[:n, c, cb * 512:(cb + 1) * 512], ps[po][:n, :512], ALU.add,
                               [Rh[c], Rps[po]], [Rh[c]])
            S.barrier()
            M.top = m0

        def attention(l, AO, RAO):
            m0 = M.top
            akT = M.alloc("akT", [128, L], BF16)
            avg = M.alloc("avg", [128, NCH, 128], BF16)
            aqT = M.alloc("aqT", [128, 4, L], BF16)
            Rak = [Res(f"ak{c}") for c in range(NCH)]
            Rav = [Res(f"av{c}") for c in range(NCH)]
            Raq = [Res(f"aq{c}") for c in range(NCH)]
            t1 = [M.alloc("at1", [128, 4, 128], F32)]
            t2 = [M.alloc("at2", [128, 4, 128], F32)]
            qn = [M.alloc("aqn", [128, 4, 128], F32)]
            qb = [M.alloc(f"aqb{i}", [128, 4, 128], BF16) for i in range(2)]
            Rt1 = [Res("at1") for _ in range(2)]
            Rt2 = [[Res(f"at2{i}") for i in range(2)] for _ in range(2)]
            Rqn = [[Res(f"aqn{i}") for i in range(4)] for _ in range(2)]
            Rqb = [Res("aqb") for _ in range(2)]
            sqa = M.alloc("sqa", [128, 4, 128], BF16)
            Rsqa = [Res(f"sqa{i}") for i in range(4)]
            ssa = M.alloc("ssa", [128, 8], F32)
            rsa = M.alloc("rsa", [128, 8], F32)
            Rssa = [[Res(f"ssa{i}") for i in range(4)] for _ in range(2)]
            Rrsa = [Res("rsa") for _ in range(2)]
            NP = 4
            m_alias = M.top
            PT = [M.alloc(f"PT{i}", [128, 512], BF16) for i in range(NP)]
            RPT = [Res("PT") for _ in range(NP)]
            rec = [M.alloc(f"rec{i}", [128, 512], F32) for i in range(2)]
            Rrec = [Res("rec") for _ in range(2)]
            m_end = M.top
            M.top = m_alias
            t1.append(M.alloc("at1b", [128, 4, 128], F32))
            t2.append(M.alloc("at2b", [128, 4, 128], F32))
            qn.append(M.alloc("aqnb", [128, 4, 128], F32))
            M.top = max(M.top, m_end)
            Rw_in = Rcast[("win", l)]
            for g in range(2):
                wt, Rw = wtile()
                wload(wt, Rw, bwin[l], Rw_in, [(0, 4096 + g * 128, 128), (128, 4352 + g * 128, 128)])
                pjb, ptb, tbs = {}, {}, {}

                def ka(c):
                    tbs[c] = load_tb(c, "ax")
                    pjb[c] = proj_tok(c, wt, Rw, 256)

                def kb(c):
                    n, b = ntok(c), c % 2
                    tbt, Rt = tbs[c]
                    pi = pjb[c]
                    rms_stats(ps[pi][:n, 0:128], n, 128, sqa[:n, 0, :], Rsqa[0], ssa[:n, b:b + 1], rsa[:n, b:b + 1], Rssa[b][0], Rrsa[b], [Rps[pi]])
                    act(avg[:n, c, :], ps[pi][:n, 128:256], AF.Copy, [Rps[pi]], [Rav[c]])
                    stt(S.dve, qn[b][:n, 0, :], ps[pi][:n, 0:128], rsa[:n, b:b + 1], gk[:n, :], ALU.mult, ALU.mult,
                        [Rps[pi], Rrsa[b], Rg], [Rqn[b][0]])
                    rope(qn[b][:n, 0:1, :], n, 1, "ax", tbt, Rt, t1[b], t2[b], Rt1[b], Rt2[b], qb[b][:n, 0:1, :], Rqb[b], [Rqn[b][0]])
                    pt = psn()
                    ptb[c] = pt
                    tr(psb[pt][:, 0:n], qb[b][:n, 0, :], n, [Rqb[b]], [Rps[pt]], True)

                def kc_(c):
                    n, cc, pt = ntok(c), col0(c), ptb[c]
                    cp(S.dve, akT[:, cc:cc + n], psb[pt][:, 0:n], [Rps[pt]], [Rak[c]])

                for i in range(NCH + 3):
                    if i < NCH:
                        ka(i)
                    if 0 <= i - 2 < NCH:
                        kb(i - 2)
                    if 0 <= i - 3 < NCH:
                        kc_(i - 3)
                wt, Rw = wtile()
                wload(wt, Rw, bwin[l], Rw_in, [(0, 3072 + g * 512, 512)])
                pjb, ptb, tbs = {}, {}, {}

                def qa(c):
                    tbs[c] = load_tb(c, "ax")
                    pjb[c] = proj_tok(c, wt, Rw, 512)

                def qb_(c):
                    n, b = ntok(c), c % 2
                    tbt, Rt = tbs[c]
                    pi = pjb[c]
                    for hh in range(4):
                        act(sqa[:n, hh, :], ps[pi][:n, hh * 128:(hh + 1) * 128], AF.Square, [Rps[pi]], [Rsqa[hh], Rssa[b][hh]],
                            accum_out=ssa[:n, 4 * b + hh:4 * b + hh + 1])
                    act(rsa[:n, 4 * b:4 * b + 4], ssa[:n, 4 * b:4 * b + 4], AF.Ln, [Rssa[b], Rc], [Rrsa[b]], scale=1.0 / 128, bias=epst[:n, :])
                    act(rsa[:n, 4 * b:4 * b + 4], rsa[:n, 4 * b:4 * b + 4], AF.Exp, [Rrsa[b]], [Rrsa[b]], scale=-0.5)
                    for hh in range(4):
                        stt(S.dve, qn[b][:n, hh, :], ps[pi][:n, hh * 128:(hh + 1) * 128], rsa[:n, 4 * b + hh:4 * b + hh + 1], gq[:n, :],
                            ALU.mult, ALU.mult, [Rps[pi], Rrsa[b], Rg], [Rqn[b][hh]])
                    rope(qn[b][:n, :, :], n, 4, "ax", tbt, Rt, t1[b], t2[b], Rt1[b], Rt2[b], qb[b][:n, :, :], Rqb[b], [Rqn[b]])
                    pt = psn()
                    ptb[c] = pt
                    for hh in range(4):
                        tr(psb[pt][:, hh * 128:hh * 128 + n], qb[b][:n, hh, :], n, [Rqb[b]], [Rps[pt]], hh == 3)

                def qc_(c):
                    n, cc, pt = ntok(c), col0(c), ptb[c]
                    cp(S.dve, aqT[:, :, cc:cc + n], psb[pt][:, 0:512].rearrange("p (j q) -> p j q", q=128)[:, :, :n],
                       [Rps[pt]], [Raq[c]])

                for i in range(NCH + 3):
                    if i < NCH:
                        qa(i)
                    if 0 <= i - 2 < NCH:
                        qb_(i - 2)
                    if 0 <= i - 3 < NCH:
                        qc_(i - 3)
                S.barrier()
                items = []
                for ti, (tc0, tn, chunks) in enumerate(TILES):
                    for hh in range(4):
                        for kc in range(NCH):
                            items.append((ti, hh, kc))
                LA = 2
                slot = {}

                def sfront(i):
                    ti, hh, kc = items[i]
                    tc0, tn, chunks = TILES[ti]
                    nk, kc0 = ntok(kc), col0(kc)
                    pS = psn(4)
                    mm(ps[pS][:nk, :tn], [(akT[:, kc0:kc0 + nk], aqT[:, hh, tc0:tc0 + tn])], [Raq[c] for c in chunks] + [Rak[kc]], [Rps[pS]])
                    pb = i % NP
                    slot[i] = pb
                    act(PT[pb][:nk, :tn], ps[pS][:nk, :tn], AF.Exp, [Rps[pS]], [RPT[pb]], scale=QK_SCALE)

                def sback(i):
                    ti, hh, kc = items[i]
                    tc0, tn, chunks = TILES[ti]
                    nk = ntok(kc)
                    hidx = ti * 4 + hh
                    po, pm = 4 + hidx % 2, 6 + hidx % 2
                    pb = slot.pop(i)
                    S.op(S.pe, lambda e: e.matmul(ps[po][:, :tn], lhsT=avg[:nk, kc, :], rhs=PT[pb][:nk, :tn],
                                                  start=(kc == 0), stop=(kc == NCH - 1)),
                         reads=[Rav[kc], RPT[pb]], writes=[Rps[po]], signal=False)
                    S.op(S.pe, lambda e: e.matmul(ps[pm][:, :tn], lhsT=ones[:nk, :], rhs=PT[pb][:nk, :tn],
                                                  start=(kc == 0), stop=(kc == NCH - 1)),
                         reads=[Rc, RPT[pb]], writes=[Rps[pm]], signal=True)
                    if kc == NCH - 1:
                        head = g * 4 + hh
                        rb = hidx % 2
                        S.op(S.dve, lambda e: e.reciprocal(out=rec[rb][:, :tn], in_=ps[pm][:, :tn]),
                             reads=[Rps[pm]], writes=[Rrec[rb]])
                        tt(S.dve, AO[:, head, tc0:tc0 + tn], ps[po][:, :tn], rec[rb][:, :tn], ALU.mult,
                           [Rps[po], Rrec[rb]], [RAO[c] for c in chunks])

                for i in range(len(items) + LA):
                    if i < len(items):
                        sfront(i)
                    if i >= LA:
                        sback(i - LA)
                if g == 0:
                    S.barrier()
            S.barrier()
            M.top = m0

        def ffn(l):
            m0 = M.top
            gT = M.alloc("gT", [128, 22, 1040], BF16)
            RgT = Res("gT")
            wfo = M.alloc("wfo", [128, 22, 512], BF16)
            Rwfo = Res("wfo")
            sa = [M.alloc(f"sa{i}", [128, 512], F32) for i in range(2)]
            Rsa = [Res("sa") for _ in range(2)]
            Rw_fi = Rcast[("wfi", l)]
            Rw_fo = Rcast[("wfo", l)]
            it = 0
            def load_wfo(cb):
                S.dma(S.sp, S.rot_chan("w", 4), wfo[:], bwfo[l][:, cb * 512:(cb + 1) * 512].rearrange("(k p) c -> p k c", p=128),
                      reads=Rw_fo, writes=[Rwfo])

            for pss in FFN_PASSES:
                tiles = [TILES[i] for i in pss]
                pc0 = tiles[0][0]
                load_wfo(0)
                for jb in range(11):
                    wa, Rwa = htile()
                    wload(wa, Rwa, bwfi[l], Rw_fi, [(0, jb * 256, 256)])
                    wu, Rwu = htile()
                    wload(wu, Rwu, bwfi[l], Rw_fi, [(0, DFF + jb * 256, 256)])
                    for jj in range(2):
                        j = jb * 2 + jj
                        for (tc0, tn, chunks) in tiles:
                            pa = psn()
                            mm(ps[pa][:, :tn], [(wa[:, k, jj * 128:(jj + 1) * 128], hnT[:, k, tc0:tc0 + tn]) for k in range(8)],
                               hn_reads(chunks) + [Rwa], [Rps[pa]])
                            pu = psn()
                            mm(ps[pu][:, :tn], [(wu[:, k, jj * 128:(jj + 1) * 128], hnT[:, k, tc0:tc0 + tn]) for k in range(8)],
                               hn_reads(chunks) + [Rwu], [Rps[pu]])
                            b = it % 2
                            it += 1
                            act(sa[b][:, :tn], ps[pa][:, :tn], AF.Silu, [Rps[pa]], [Rsa[b]])
                            tt(S.dve, gT[:, j, tc0 - pc0:tc0 - pc0 + tn], ps[pu][:, :tn], sa[b][:, :tn], ALU.mult,
                               [Rps[pu], Rsa[b]], [RgT])
                for cb in range(2):
                    if cb == 1:
                        load_wfo(1)
                    for (tc0, tn, chunks) in tiles:
                        for c in chunks:
                            n, lc = ntok(c), col0(c) - pc0
                            po = psn()
                            mm(ps[po][:n, :512], [(gT[:, k, lc:lc + n], wfo[:, k, :]) for k in range(22)], [RgT, Rwfo], [Rps[po]])
                            tt(S.dve, h[:n, c, cb * 512:(cb + 1) * 512], h[:n, c, cb * 512:(cb + 1) * 512], ps[po][:n, :512], ALU.add,
                               [Rh[c], Rps[po]], [Rh[c]])
            S.barrier()
            M.top = m0

        def final_phase(s):
            m0 = M.top
            gbf = M.alloc("gbf", [128, D], F32)
            Rgbf = Res("gbf")
            yo = [M.alloc(f"yo{i}", [128, D], F32) for i in range(2)]
            Ryo = [Res("yo") for _ in range(2)]
            sqj = M.alloc("sqjf", [128, D], BF16)
            Rsqj = Res("sqjf")
            S.dma(S.sp, S.chan("g"), gbf[:], nfin_d.partition_broadcast(128), writes=[Rgbf])
            for c in range(1, NCH):
                b = c % 2
                rms_stats(h[:, c, :], 128, D, sqj[:, :], Rsqj, ssn[:, c:c + 1], rsn[:, c:c + 1], Rssn[c], Rrsn[c], [Rh[c]])
                stt(S.dve, yo[b][:], h[:, c, :], rsn[:, c:c + 1], gbf[:], ALU.mult, ALU.mult, [Rh[c], Rrsn[c], Rgbf], [Ryo[b]])
                S.dma(S.sp, S.rot_chan("y", 2), y_d[s, (c - 1) * 128:c * 128, :], yo[b][:], reads=[Ryo[b]])
            S.barrier()
            M.top = m0

        RXT = [Res(f"XT{c}") for c in range(NCH)]
        for s in range(NS):
            S.dma(S.sp, S.rot_chan("x", 2), h[:NMETA, 0, :], meta_d, writes=[Rh[0]])
            for t in range(4):
                S.dma(S.sp, S.rot_chan("x", 2), h[:, 1 + 4 * t:5 + 4 * t, :],
                      x_d[s, 512 * t:512 * (t + 1), :].rearrange("(c p) d -> p c d", p=128),
                      writes=[Rh[1 + 4 * t + i] for i in range(4)])
            for l in range(DEPTH):
                norm_phase(nmix_d[l])
                dec_tables(l)
                m_mix = M.top
                XT = M.alloc("XT", [128, 8, L], BF16)
                retention(l, XT, RXT)
                gate_rg(l, XT, RXT)
                epilogue(l, XT, RXT, bwro, Rcast[("wro", l)], 4608)
                attention(l, XT, RXT)
                epilogue(l, XT, RXT, bwao, Rcast[("wao", l)], 5632)
                M.top = m_mix
                norm_phase(nffn_d[l])
                ffn(l)
            final_phase(s)
        S.wait_all(S.sp)
        S.emit()
    return nc


_TB_CACHE = {}


def rope_tables():
    if "tb" in _TB_CACHE:
        return _TB_CACHE["tb"]
    f32 = np.float32
    ret_inv = (f32(10000.0) ** (-np.linspace(0.0, 1.0, 64, dtype=f32))).astype(f32)
    ret_ang = (np.arange(L, dtype=f32)[:, None] * ret_inv[None, :]).astype(f32)
    rows = SEQ // 64
    zeros = np.zeros((NMETA,), f32)
    row_ids = np.concatenate([zeros, np.repeat(np.arange(rows, dtype=f32), 64)])
    col_ids = np.concatenate([zeros, np.tile(np.arange(64, dtype=f32), rows)])
    ax_inv = (f32(10000.0) ** (-np.arange(32, dtype=f32) * f32(2.0) / f32(64))).astype(f32)
    row_ang = (row_ids[:, None] * ax_inv[None, :]).astype(f32)
    col_ang = (col_ids[:, None] * ax_inv[None, :]).astype(f32)
    tb = np.zeros((L, 4, 128), f32)
    c, s = np.cos(ret_ang), np.sin(ret_ang)
    tb[:, 0, :64], tb[:, 0, 64:] = c, c
    tb[:, 1, :64], tb[:, 1, 64:] = -s, s
    cr, sr, cc, sc = np.cos(row_ang), np.sin(row_ang), np.cos(col_ang), np.sin(col_ang)
    tb[:, 2, 0:32], tb[:, 2, 32:64], tb[:, 2, 64:96], tb[:, 2, 96:128] = cr, cr, cc, cc
    tb[:, 3, 0:32], tb[:, 3, 32:64], tb[:, 3, 64:96], tb[:, 3, 96:128] = -sr, sr, -sc, sc
    _TB_CACHE["tb"] = tb
    return tb


def make_in_maps(xs_per_core, meta_tokens, norm_mix, w_in, ret_decay, q_norm, k_norm, w_ret_o, w_att_o, w_out,
                 norm_ffn, w_ffn_in, w_ffn_out, norm_final):
    c = lambda a: np.ascontiguousarray(np.asarray(a, dtype=np.float32))
    depth = np.asarray(w_in).shape[0]
    shared = {
        "meta": c(meta_tokens), "norm_mix": c(norm_mix), "w_in": c(w_in),
        "ret_decay": c(np.asarray(ret_decay).reshape(depth, 8)), "q_norm": c(q_norm), "k_norm": c(k_norm),
        "w_ret_o": c(w_ret_o), "w_att_o": c(w_att_o), "w_out": c(w_out), "norm_ffn": c(norm_ffn),
        "w_ffn_in": c(w_ffn_in), "w_ffn_out": c(w_ffn_out), "norm_final": c(norm_final), "rope_tb": rope_tables(),
    }
    return [dict(shared, x=c(xc)) for xc in xs_per_core]


def kernel(x_prompt, x_sample, meta_tokens, norm_mix, w_in, ret_decay, q_norm, k_norm, w_ret_o, w_att_o, w_out,
           norm_ffn, w_ffn_in, w_ffn_out, norm_final):
    xp = np.asarray(x_prompt, dtype=np.float32)
    xs = np.asarray(x_sample, dtype=np.float32)
    allx = np.concatenate([xp, xs], axis=0)
    nseq = allx.shape[0]
    per = nseq // N_CORES
    depth = np.asarray(w_in).shape[0]
    nc = bass.Bass("TRN2", target_bir_lowering=False)
    build_program(nc, per, depth)
    in_maps = make_in_maps([allx[i * per:(i + 1) * per] for i in range(N_CORES)], meta_tokens, norm_mix, w_in, ret_decay,
                           q_norm, k_norm, w_ret_o, w_att_o, w_out, norm_ffn, w_ffn_in, w_ffn_out, norm_final)
    res = run_bass_kernel_spmd(nc, in_maps, core_ids=list(range(N_CORES)))
    y = np.concatenate([np.asarray(r["y"], dtype=np.float32) for r in res.results], axis=0)
    return (y[:xp.shape[0]], y[xp.shape[0]:])
```

```python
import contextlib
import numpy as np
import concourse.bass as bass
import concourse.mybir as mybir

F32 = mybir.dt.float32
BF16 = mybir.dt.bfloat16
ALU = mybir.AluOpType
AF = mybir.ActivationFunctionType
AX = mybir.AxisListType

SEM_ROT = 8000


def _flat(rs):
    out = []
    for r in rs:
        if isinstance(r, (list, tuple)):
            out.extend(_flat(r))
        else:
            out.append(r)
    return out


class Res:
    __slots__ = ("name", "w", "r", "excl")

    def __init__(self, name):
        self.name = name
        self.excl = False
        self.w = None
        self.r = {}


class Prod:
    def __init__(self, sched, name, step):
        self.sched = sched
        self.name = name
        self.step = step
        self.epoch = 0
        self.count = 0
        self.sems = [sched.new_sem(f"{name}_0")]

    def _maybe_rotate(self):
        if self.count >= SEM_ROT:
            self.epoch += 1
            self.count = 0
            self.sems.append(self.sched.new_sem(f"{self.name}_{self.epoch}"))

    def bump(self):
        self._maybe_rotate()
        self.count += self.step
        return (self, self.epoch, self.count)

    def cur(self):
        self._maybe_rotate()
        return (self, self.epoch, self.count)


class Eng(Prod):
    def __init__(self, sched, name):
        super().__init__(sched, name, 1)
        self.ops = []
        self.waited = {}


class Sched:
    def __init__(self, nc):
        self.nc = nc
        self.stack = contextlib.ExitStack()
        self.nsem = 0
        self.pe = Eng(self, "pe")
        self.act = Eng(self, "act")
        self.dve = Eng(self, "dve")
        self.pool = Eng(self, "pool")
        self.sp = Eng(self, "sp")
        self.engs = [self.pe, self.act, self.dve, self.pool, self.sp]
        self.chans = {}
        self.rot = {}
        self.n_ops = 0

    def new_sem(self, name):
        self.nsem += 1
        return self.stack.enter_context(self.nc.semaphore(name))

    def sbuf(self, name, shape, dtype):
        return self.stack.enter_context(self.nc.sbuf_tensor(name, shape, dtype))

    def psum(self, name, shape, dtype):
        return self.stack.enter_context(self.nc.psum_tensor(name, shape, dtype))

    def _deps(self, eng, reads, writes):
        deps = {}
        strict = eng.name != "pe"

        def add(t):
            p, e, c = t
            k = (p, e)
            if deps.get(k, 0) < c:
                deps[k] = c

        for r in reads:
            if r.w is not None:
                add(r.w)
            if r.excl:
                for (p, e), c in r.r.items():
                    if p is not eng:
                        add((p, e, c))
        for w in writes:
            if w.w is not None and (w.w[0] is not eng or strict):
                add(w.w)
            for (p, e), c in w.r.items():
                if p is eng and not strict:
                    continue
                add((p, e, c))
        waits = []
        for (p, e), c in deps.items():
            k = (p.name, e)
            if eng.waited.get(k, 0) < c:
                eng.waited[k] = c
                waits.append((p.sems[e], c))
        return waits

    def _mark(self, tok, reads, writes):
        p, e, c = tok
        for r in reads:
            k = (p, e)
            if r.r.get(k, 0) < c:
                r.r[k] = c
        for w in writes:
            w.w = tok
            w.r = {}

    def op(self, eng, fn, reads=(), writes=(), signal=True):
        reads, writes = _flat(reads), _flat(writes)
        waits = self._deps(eng, reads, writes)
        if signal:
            tok = eng.bump()
            inc = (tok[0].sems[tok[1]], 1)
        else:
            p, e, c = eng.cur()
            tok = (p, e, c + 1)
            inc = None
        eng.ops.append((waits, fn, inc))
        self._mark(tok, reads, writes)
        self.n_ops += 1

    def chan(self, name):
        if name not in self.chans:
            self.chans[name] = Prod(self, "d" + name, 16)
        return self.chans[name]

    def rot_chan(self, group, n):
        i = self.rot.get(group, 0)
        self.rot[group] = i + 1
        return self.chan(f"{group}{i % n}")

    def dma(self, q, ch, out_ap, in_ap, reads=(), writes=(), **kw):
        reads, writes = _flat(reads), _flat(writes)
        waits = self._deps(q, reads, writes)
        k = (ch.name, ch.epoch)
        if ch.count > 0 and q.waited.get(k, 0) < ch.count:
            q.waited[k] = ch.count
            waits.append((ch.sems[ch.epoch], ch.count))
        tok = ch.bump()
        inc = (tok[0].sems[tok[1]], 16)
        q.ops.append((waits, lambda e: e.dma_start(out=out_ap, in_=in_ap, **kw), inc))
        self._mark(tok, reads, writes)
        self.n_ops += 1

    def barrier(self):
        prods = list(self.engs) + list(self.chans.values())
        for eng in self.engs:
            waits = []
            for p in prods:
                if p.count == 0 and p.epoch == 0:
                    continue
                k = (p.name, p.epoch)
                if eng.waited.get(k, 0) < p.count:
                    eng.waited[k] = p.count
                    waits.append((p.sems[p.epoch], p.count))
            if waits:
                eng.ops.append((waits, None, None))

    def wait_all(self, eng):
        prods = list(self.engs) + list(self.chans.values())
        waits = []
        for p in prods:
            if p is eng or (p.count == 0 and p.epoch == 0):
                continue
            waits.append((p.sems[p.epoch], p.count))
        eng.ops.append((waits, None, None))

    def emit(self):
        nc = self.nc
        with nc.Block() as block:
            def replay(eng):
                def body(e):
                    for waits, fn, inc in eng.ops:
                        for sem, val in waits:
                            e.wait_ge(sem, val)
                        if fn is None:
                            continue
                        ins = fn(e)
                        if inc is not None:
                            ins.then_inc(inc[0], inc[1])
                return body

            block.tensor(replay(self.pe))
            block.scalar(replay(self.act))
            block.vector(replay(self.dve))
            block.gpsimd(replay(self.pool))
            block.sync(replay(self.sp))

from concourse.bass_utils import run_bass_kernel_spmd

D = 1024
SEQ = 2048
NMETA = 16
L = SEQ + NMETA
NCH = 17
INW = 6656
DFF = 2816
LN2 = 0.6931471805599453
QK_SCALE = 128.0 ** -0.5
EPS = 1e-6
N_CORES = 8


def ntok(c):
    return 16 if c == 0 else 128


def col0(c):
    return 0 if c == 0 else 16 + (c - 1) * 128


TILES = [(0, 16, [0])] + [(16 + 512 * t, 512, [1 + 4 * t + i for i in range(4)]) for t in range(4)]
PASSES = [[0, 1], [2], [3], [4]]
FFN_PASSES = [[0, 1, 2], [3, 4]]


class Mem:
    def __init__(self, nc, base=16640, limit=229376):
        self.nc = nc
        self.top = base
        self.limit = limit
        self.n = 0

    def alloc(self, name, shape, dtype):
        esz = 4 if dtype == F32 else 2
        size = esz
        for s in shape[1:]:
            size *= s
        off = (self.top + 31) // 32 * 32
        self.top = off + size
        assert self.top <= self.limit, f"SBUF overflow at {name}: {self.top}"
        self.n += 1
        return self.nc.alloc_sbuf_tensor_at(f"{name}_{self.n}", list(shape), dtype, offset=off)


def build_program(nc, NS, DEPTH):
    dr = lambda name, shape, dt, kind: nc.dram_tensor(name, shape, dt, kind=kind).ap()
    x_d = dr("x", [NS, SEQ, D], F32, "ExternalInput")
    meta_d = dr("meta", [NMETA, D], F32, "ExternalInput")
    nmix_d = dr("norm_mix", [DEPTH, D], F32, "ExternalInput")
    win_d = dr("w_in", [DEPTH, D, INW], F32, "ExternalInput")
    dec_d = dr("ret_decay", [DEPTH, 8], F32, "ExternalInput")
    qn_d = dr("q_norm", [DEPTH, 128], F32, "ExternalInput")
    kn_d = dr("k_norm", [DEPTH, 128], F32, "ExternalInput")
    wro_d = dr("w_ret_o", [DEPTH, D, D], F32, "ExternalInput")
    wao_d = dr("w_att_o", [DEPTH, D, D], F32, "ExternalInput")
    wo_d = dr("w_out", [DEPTH, D, D], F32, "ExternalInput")
    nffn_d = dr("norm_ffn", [DEPTH, D], F32, "ExternalInput")
    wfi_d = dr("w_ffn_in", [DEPTH, D, 2 * DFF], F32, "ExternalInput")
    wfo_d = dr("w_ffn_out", [DEPTH, DFF, D], F32, "ExternalInput")
    nfin_d = dr("norm_final", [D], F32, "ExternalInput")
    tb_d = dr("rope_tb", [L, 4, 128], F32, "ExternalInput")
    y_d = dr("y", [NS, SEQ, D], F32, "ExternalOutput")
    bwin = dr("b_w_in", [DEPTH, D, INW], BF16, "Internal")
    bwro = dr("b_w_ret_o", [DEPTH, D, D], BF16, "Internal")
    bwao = dr("b_w_att_o", [DEPTH, D, D], BF16, "Internal")
    bwo = dr("b_w_out", [DEPTH, D, D], BF16, "Internal")
    bwfi = dr("b_w_ffn_in", [DEPTH, D, 2 * DFF], BF16, "Internal")
    bwfo = dr("b_w_ffn_out", [DEPTH, DFF, D], BF16, "Internal")

    S = Sched(nc)
    M = Mem(nc)
    with S.stack:
        ps = [S.psum(f"ps{i}", [128, 512], F32) for i in range(8)]
        psb = [p.bitcast(BF16) for p in ps]
        Rps = [Res(f"ps{i}") for i in range(8)]
        for r_ in Rps:
            r_.excl = True
        st = {"psi": 0, "wi": 0}

        def psn(nb=8):
            i = st["psi"] % nb
            st["psi"] += 1
            return i

        h = M.alloc("h", [128, NCH, D], F32)
        Rh = [Res(f"h{c}") for c in range(NCH)]
        hnT = M.alloc("hnT", [128, 8, L], BF16)
        RhA = [[Res(f"hnT{c}_{k}") for k in range(8)] for c in range(NCH)]
        NW = 2
        wring = [M.alloc(f"wr{i}", [128, 8, 512], BF16) for i in range(NW)]
        Rwh = [Res(f"wrh{i}") for i in range(2 * NW)]
        NTB = 4
        tbuf = [M.alloc(f"tb{i}", [128, 2, 128], F32) for i in range(NTB)]
        Rtb = [Res(f"tb{i}") for i in range(NTB)]
        ident = M.alloc("ident", [128, 128], BF16)
        ones = M.alloc("ones", [128, 128], BF16)
        epst = M.alloc("eps", [128, 1], F32)
        cst = M.alloc("cst", [128, 8, 128], F32)
        pcs = M.alloc("pcs", [128, 4], F32)
        DT = M.alloc("DT", [128, 4, 128], F32)
        qdf = M.alloc("qdf", [128, 4, 128], F32)
        qdb = M.alloc("qdb", [128, 4, 128], F32)
        dect = M.alloc("dect", [128, 8], F32)
        lg = M.alloc("lg", [128, 8], F32)
        gch = M.alloc("gch", [128, 8], F32)
        kdfs = M.alloc("kdfs", [128, 4], F32)
        kdbs = M.alloc("kdbs", [128, 4], F32)
        kdf0s = M.alloc("kdf0s", [128, 4], F32)
        gq = M.alloc("gq", [128, 128], F32)
        gk = M.alloc("gk", [128, 128], F32)
        ssn = M.alloc("ssn", [128, NCH], F32)
        rsn = M.alloc("rsn", [128, NCH], F32)
        Rssn = [Res(f"ssn{c}") for c in range(NCH)]
        Rrsn = [Res(f"rsn{c}") for c in range(NCH)]
        Rc = Res("consts")
        Rtab = Res("dectables")
        Rg = Res("gqk")
        PH0 = M.top

        def mm(out_ap, pairs, reads, writes):
            n = len(pairs)
            for i, (l_, r_) in enumerate(pairs):
                S.op(S.pe, lambda e, l_=l_, r_=r_, i=i: e.matmul(out_ap, lhsT=l_, rhs=r_, start=(i == 0), stop=(i == n - 1)),
                     reads=reads, writes=writes, signal=(i == n - 1))

        def tr(out_ap, in_ap, n, reads, writes, last):
            S.op(S.pe, lambda e: e.transpose(out=out_ap, in_=in_ap, identity=ident[:n, :n]),
                 reads=list(reads) + [Rc], writes=writes, signal=last)

        def act(out, in_, func, reads, writes, **kw):
            S.op(S.act, lambda e: e.activation(out=out, in_=in_, func=func, **kw), reads=reads, writes=writes)

        def tt(eng, out, in0, in1, op, reads, writes):
            S.op(eng, lambda e: e.tensor_tensor(out=out, in0=in0, in1=in1, op=op), reads=reads, writes=writes)

        def tsc(eng, out, in0, s1, op0, reads, writes, s2=None, op1=None):
            if op1 is None:
                S.op(eng, lambda e: e.tensor_scalar(out=out, in0=in0, scalar1=s1, scalar2=None, op0=op0), reads=reads, writes=writes)
            else:
                S.op(eng, lambda e: e.tensor_scalar(out=out, in0=in0, scalar1=s1, scalar2=s2, op0=op0, op1=op1), reads=reads, writes=writes)

        def stt(eng, out, in0, sc, in1, op0, op1, reads, writes):
            S.op(eng, lambda e: e.scalar_tensor_tensor(out=out, in0=in0, scalar=sc, in1=in1, op0=op0, op1=op1), reads=reads, writes=writes)

        def cp(eng, out, in_, reads, writes):
            S.op(eng, lambda e: e.tensor_copy(out=out, in_=in_), reads=reads, writes=writes)

        def hn_reads(chunks):
            return [RhA[c] for c in chunks]

        def wtile():
            i = st["wi"] % NW
            st["wi"] += 1
            return wring[i], [Rwh[2 * i], Rwh[2 * i + 1]]

        def htile():
            j = st.get("hi", 0) % (2 * NW)
            st["hi"] = st.get("hi", 0) + 1
            return wring[j // 2][:, :, (j % 2) * 256:(j % 2 + 1) * 256], [Rwh[j]]

        def wload(dst, Rdst, src, Rsrc, pieces):
            ch = S.rot_chan("w", 4)
            for (d0, s0, ncol) in pieces:
                S.dma(S.sp, ch, dst[:, :, d0:d0 + ncol], src[:, s0:s0 + ncol].rearrange("(k p) c -> p k c", p=128),
                      reads=Rsrc, writes=[Rdst])

        Rcast = {}

        def cast(name, dst, src, l, rows):
            rl = []
            for r0 in range(0, rows, 128):
                r = Res(f"{name}{l}_{r0}")
                S.dma(S.pool, S.rot_chan("c", 4), dst[l, r0:r0 + 128, :], src[l, r0:r0 + 128, :], writes=[r])
                rl.append(r)
            Rcast[(name, l)] = rl

        S.op(S.pool, lambda e: e.memset(ident[:], 1.0), writes=[Rc])
        S.op(S.pool, lambda e: e.affine_select(out=ident[:], in_=ident[:], pattern=[[-1, 128]], compare_op=ALU.is_equal,
                                               fill=0.0, base=0, channel_multiplier=1), reads=[Rc], writes=[Rc])
        S.op(S.pool, lambda e: e.memset(ones[:], 1.0), writes=[Rc])
        S.op(S.pool, lambda e: e.memset(epst[:], EPS), writes=[Rc])
        S.op(S.pool, lambda e: e.iota(cst[:, 6, :], pattern=[[1, 128]], base=0, channel_multiplier=-1,
                                      allow_small_or_imprecise_dtypes=True), writes=[Rc])
        tsc(S.dve, cst[:, 0, :], cst[:, 6, :], 0.0, ALU.max, [Rc], [Rc])
        tsc(S.dve, cst[:, 1, :], cst[:, 6, :], -1.0, ALU.mult, [Rc], [Rc], 0.0, ALU.max)
        tsc(S.dve, cst[:, 2, :], cst[:, 6, :], 0.0, ALU.is_ge, [Rc], [Rc], QK_SCALE, ALU.mult)
        tsc(S.dve, cst[:, 3, :], cst[:, 6, :], 0.0, ALU.is_lt, [Rc], [Rc], QK_SCALE, ALU.mult)
        S.op(S.pool, lambda e: e.iota(cst[:, 4, :], pattern=[[1, 128]], base=1, channel_multiplier=0,
                                      allow_small_or_imprecise_dtypes=True), reads=[Rc], writes=[Rc])
        S.op(S.pool, lambda e: e.iota(cst[:, 5, :], pattern=[[-1, 128]], base=128, channel_multiplier=0,
                                      allow_small_or_imprecise_dtypes=True), reads=[Rc], writes=[Rc])
        S.op(S.pool, lambda e: e.iota(pcs[:, 0:1], pattern=[[0, 1]], base=0, channel_multiplier=1,
                                      allow_small_or_imprecise_dtypes=True), reads=[Rc], writes=[Rc])
        S.op(S.pool, lambda e: e.iota(pcs[:, 1:2], pattern=[[0, 1]], base=127, channel_multiplier=-1,
                                      allow_small_or_imprecise_dtypes=True), reads=[Rc], writes=[Rc])
        S.op(S.pool, lambda e: e.iota(pcs[:, 2:3], pattern=[[0, 1]], base=15, channel_multiplier=-1,
                                      allow_small_or_imprecise_dtypes=True), reads=[Rc], writes=[Rc])

        for l in range(DEPTH):
            cast("win", bwin, win_d, l, D)
            cast("wro", bwro, wro_d, l, D)
            cast("wo", bwo, wo_d, l, D)
            cast("wao", bwao, wao_d, l, D)
            cast("wfi", bwfi, wfi_d, l, D)
            cast("wfo", bwfo, wfo_d, l, DFF)

        def load_tb(c, kind):
            i = st.get("tbi", 0) % NTB
            st["tbi"] = st.get("tbi", 0) + 1
            n = ntok(c)
            p0 = 0 if kind == "ret" else 2
            S.dma(S.sp, S.rot_chan("t", 3), tbuf[i][:n], tb_d[col0(c):col0(c) + n, p0:p0 + 2, :], writes=[Rtb[i]])
            return tbuf[i], Rtb[i]

        def rms_stats(src_ap, n, width, junk, Rjunk, ss_ap, rs_ap, Rss, Rrs, reads):
            act(junk, src_ap, AF.Square, reads, [Rjunk, Rss], accum_out=ss_ap)
            act(rs_ap, ss_ap, AF.Ln, [Rss, Rc], [Rrs], scale=1.0 / width, bias=epst[:n, :])
            act(rs_ap, rs_ap, AF.Exp, [Rrs], [Rrs], scale=-0.5)

        def norm_phase(gain_row):
            m0 = M.top
            hs = [M.alloc(f"hs{i}", [128, D], BF16) for i in range(2)]
            Rhs = [Res(f"hs{i}") for i in range(2)]
            sqj = M.alloc("sqj", [128, D], BF16)
            Rsqj = Res("sqj")
            gb = M.alloc("gbn", [128, D], F32)
            Rgb = Res("gbn")
            S.dma(S.sp, S.chan("g"), gb[:], gain_row.partition_broadcast(128), writes=[Rgb])
            pbank = {}

            def front(c):
                n, b = ntok(c), c % 2
                rms_stats(h[:n, c, :], n, D, sqj[:n, :], Rsqj, ssn[:n, c:c + 1], rsn[:n, c:c + 1], Rssn[c], Rrsn[c], [Rh[c]])
                stt(S.dve, hs[b][:n, :], h[:n, c, :], rsn[:n, c:c + 1], gb[:n, :], ALU.mult, ALU.mult, [Rh[c], Rrsn[c], Rgb], [Rhs[b]])
                pi = psn()
                pbank[c] = pi
                for k in range(8):
                    tr(psb[pi][:, k * 128:k * 128 + n], hs[b][:n, k * 128:(k + 1) * 128], n, [Rhs[b]], [Rps[pi]], k == 7)

            def back(c):
                n, cc, pi = ntok(c), col0(c), pbank[c]
                src = psb[pi][:, 0:1024].rearrange("p (k q) -> p k q", q=128)[:, :, :n]
                if c % 2 == 0:
                    act(hnT[:, :, cc:cc + n], src, AF.Copy, [Rps[pi]], [RhA[c]])
                else:
                    cp(S.dve, hnT[:, :, cc:cc + n], src, [Rps[pi]], [RhA[c]])

            front(0)
            if NCH > 1:
                front(1)
            for c in range(NCH):
                if c + 2 < NCH:
                    front(c + 2)
                back(c)
            S.barrier()
            M.top = m0

        def proj_tok(c, wt, Rw, ncols):
            n = ntok(c)
            cc = col0(c)
            pi = psn()
            mm(ps[pi][:n, :ncols], [(hnT[:, k, cc:cc + n], wt[:, k, :ncols]) for k in range(8)],
               hn_reads([c]) + [Rw], [Rps[pi]])
            return pi

        def rope(src, n, H, kind, tbt, Rt, t1, t2, Rt1, Rt2, out, Rout, src_reads):
            ci, si = 0, 1
            Cb = tbt[:n, ci:ci + 1, :].to_broadcast([n, H, 128])
            tt(S.dve, t1[:n, :H, :], src, Cb, ALU.mult, src_reads + [Rt], [Rt1])
            if kind == "ret":
                for hf in range(2):
                    o0, i0 = hf * 64, (1 - hf) * 64
                    Sb = tbt[:n, si:si + 1, o0:o0 + 64].to_broadcast([n, H, 64])
                    tt(S.dve, t2[:n, :H, o0:o0 + 64], src[:, :, i0:i0 + 64], Sb, ALU.mult, src_reads + [Rt], [Rt2[hf]])
            else:
                sv = src.rearrange("p h (a b c) -> p h a b c", a=2, b=2)
                tv = t2[:n, :H, :].rearrange("p h (a b c) -> p h a b c", a=2, b=2)
                Sv = tbt[:n, si, :].rearrange("p (a b c) -> p a b c", a=2, b=2)
                for b in range(2):
                    Sb = Sv[:, :, b, :].unsqueeze(1).to_broadcast([n, H, 2, 32])
                    tt(S.dve, tv[:, :, :, b, :], sv[:, :, :, 1 - b, :], Sb, ALU.mult, src_reads + [Rt], [Rt2[b]])
            tt(S.pool, out, t1[:n, :H, :], t2[:n, :H, :], ALU.add, [Rt1, Rt2], [Rout])

        def dec_tables(l):
            S.dma(S.sp, S.chan("g"), dect[:], dec_d[l].partition_broadcast(128), writes=[Rtab])
            S.dma(S.sp, S.chan("g"), gq[:], qn_d[l].partition_broadcast(128), writes=[Rg])
            S.dma(S.sp, S.chan("g"), gk[:], kn_d[l].partition_broadcast(128), writes=[Rg])
            T = [Rtab]
            act(lg[:], dect[:], AF.Exp, T, T, scale=-LN2)
            tsc(S.dve, lg[:], lg[:], -1.0, ALU.mult, T, T, 1.0, ALU.add)
            act(lg[:], lg[:], AF.Ln, T, T)
            act(gch[:], lg[:], AF.Exp, T, T, scale=128.0)
            for hd in range(4):
                act(cst[:, 6, :], cst[:, 0, :], AF.Exp, T + [Rc], T, scale=lg[:, hd:hd + 1])
                act(cst[:, 7, :], cst[:, 1, :], AF.Exp, T + [Rc], T, scale=lg[:, 4 + hd:5 + hd])
                tt(S.dve, cst[:, 6, :], cst[:, 6, :], cst[:, 2, :], ALU.mult, T + [Rc], T)
                tt(S.dve, cst[:, 7, :], cst[:, 7, :], cst[:, 3, :], ALU.mult, T + [Rc], T)
                tt(S.dve, DT[:, hd, :], cst[:, 6, :], cst[:, 7, :], ALU.add, T, T)
                act(qdf[:, hd, :], cst[:, 4, :], AF.Exp, T + [Rc], T, scale=lg[:, hd:hd + 1])
                act(qdb[:, hd, :], cst[:, 5, :], AF.Exp, T + [Rc], T, scale=lg[:, 4 + hd:5 + hd])
                act(kdfs[:, hd:hd + 1], pcs[:, 1:2], AF.Exp, T + [Rc], T, scale=lg[:, hd:hd + 1])
                act(kdbs[:, hd:hd + 1], pcs[:, 0:1], AF.Exp, T + [Rc], T, scale=lg[:, 4 + hd:5 + hd])
                act(kdf0s[:, hd:hd + 1], pcs[:, 2:3], AF.Exp, T + [Rc], T, scale=lg[:, hd:hd + 1])
            for t_ in (kdfs, kdbs, kdf0s):
                tsc(S.dve, t_[:], t_[:], QK_SCALE, ALU.mult, T, T)

        def retention(l, XT, RXT):
            m0 = M.top
            rqkT = M.alloc("rqkT", [128, 2, L], BF16)
            kdfh = M.alloc("kdfh", [128, NCH, 128], BF16)
            rvh = M.alloc("rvh", [128, NCH, 256], BF16)
            sball = M.alloc("sball", [128, NCH, 256], BF16)
            Rqk = [Res(f"rqk{c}") for c in range(NCH)]
            Rkd = [Res(f"kdf{c}") for c in range(NCH)]
            Rrv = [Res(f"rv{c}") for c in range(NCH)]
            Rsb = [Res(f"sb{c}") for c in range(NCH)]
            t1 = [M.alloc(f"rt1{i}", [128, 2, 128], F32) for i in range(2)]
            t2 = [M.alloc(f"rt2{i}", [128, 2, 128], F32) for i in range(2)]
            qkb = [M.alloc(f"qkb{i}", [128, 2, 128], BF16) for i in range(2)]
            kdb = [M.alloc(f"kdb{i}", [128, 128], BF16) for i in range(2)]
            Rt1 = [Res("t1") for _ in range(2)]
            Rt2 = [[Res("t2a"), Res("t2b")] for _ in range(2)]
            Rqkb = [Res("qkb") for _ in range(2)]
            Rkdb = [Res("kdb") for _ in range(2)]
            sb32 = M.alloc("sb32", [128, 256], F32)
            sf32 = M.alloc("sf32", [128, 256], F32)
            Rsb32, Rsf32 = Res("sb32"), Res("sf32")
            sfb = [M.alloc(f"sfb{i}", [128, 256], BF16) for i in range(2)]
            Rsfb = [Res("sfb") for _ in range(2)]
            AT = [M.alloc(f"AT{i}", [128, 128], BF16) for i in range(2)]
            QfT = [M.alloc(f"QfT{i}", [128, 128], BF16) for i in range(2)]
            QbT = [M.alloc(f"QbT{i}", [128, 128], BF16) for i in range(2)]
            RAT = [Res("AT") for _ in range(2)]
            RQf = [Res("Qf") for _ in range(2)]
            RQb = [Res("Qb") for _ in range(2)]
            sqo = M.alloc("sqo", [128, 256], BF16)
            Rsqo = Res("sqo")
            on = [M.alloc(f"on{i}", [128, 256], BF16) for i in range(2)]
            Ron = [Res("on") for _ in range(2)]
            sso = M.alloc("sso", [128, 2], F32)
            rso = M.alloc("rso", [128, 2], F32)
            Rsso = [Res("sso") for _ in range(2)]
            Rrso = [Res("rso") for _ in range(2)]
            Rw_in = Rcast[("win", l)]
            for hd in range(4):
                wt, Rw = wtile()
                wload(wt, Rw, bwin[l], Rw_in, [(0, hd * 128, 128), (128, 512 + hd * 128, 128), (256, 1024 + hd * 256, 256)])
                S.op(S.pool, lambda e: e.memset(sb32[:], 0.0), writes=[Rsb32])
                S.op(S.pool, lambda e: e.memset(sball[:, NCH - 1, :], 0.0), writes=[Rsb[NCH - 1]])
                pjb, ptb, tbs = {}, {}, {}

                def s1a(c):
                    tbs[c] = load_tb(c, "ret")
                    pjb[c] = proj_tok(c, wt, Rw, 512)

                def s1b(c):
                    n, b = ntok(c), c % 2
                    tbt, Rt = tbs[c]
                    pi = pjb[c]
                    src = ps[pi][:n, 0:256].rearrange("p (h d) -> p h d", h=2)
                    rope(src, n, 2, "ret", tbt, Rt, t1[b], t2[b], Rt1[b], Rt2[b], qkb[b][:n], Rqkb[b], [Rps[pi]])
                    cp(S.dve, rvh[:n, c, :], ps[pi][:n, 256:512], [Rps[pi]], [Rrv[c]])
                    ksc = kdf0s if c == 0 else kdfs
                    act(kdfh[:n, c, :], qkb[b][:n, 1, :], AF.Copy, [Rqkb[b], Rtab], [Rkd[c]], scale=ksc[:n, hd:hd + 1])
                    pt = psn()
                    ptb[c] = pt
                    for j in range(2):
                        tr(psb[pt][:, j * 128:j * 128 + n], qkb[b][:n, j, :], n, [Rqkb[b]], [Rps[pt]], j == 1)

                def s1c(c):
                    n, cc, b, pt = ntok(c), col0(c), c % 2, ptb[c]
                    if c >= 1:
                        tsc(S.dve, kdb[b][:n, :], qkb[b][:n, 1, :], kdbs[:n, hd:hd + 1], ALU.mult, [Rqkb[b], Rtab], [Rkdb[b]])
                    cp(S.dve, rqkT[:, :, cc:cc + n], psb[pt][:, 0:256].rearrange("p (j q) -> p j q", q=128)[:, :, :n],
                       [Rps[pt]], [Rqk[c]])
                    if c >= 1:
                        pd = psn()
                        mm(ps[pd][:, :256], [(kdb[b][:n, :], rvh[:n, c, :])], [Rkdb[b], Rrv[c]], [Rps[pd]])
                        stt(S.dve, sb32[:], sb32[:], gch[:, 4 + hd:5 + hd], ps[pd][:, :256], ALU.mult, ALU.add,
                            [Rsb32, Rps[pd], Rtab], [Rsb32])
                        act(sball[:, c - 1, :], sb32[:], AF.Copy, [Rsb32], [Rsb[c - 1]])

                order = list(range(NCH - 1, -1, -1))
                for i in range(NCH + 3):
                    if i < NCH:
                        s1a(order[i])
                    if 0 <= i - 2 < NCH:
                        s1b(order[i - 2])
                    if 0 <= i - 3 < NCH:
                        s1c(order[i - 3])
                S.op(S.pool, lambda e: e.memset(sf32[:], 0.0), writes=[Rsf32])

                def s3a(c):
                    n, cc, b = ntok(c), col0(c), c % 2
                    off = 112 if c == 0 else 0
                    p1 = psn()
                    mm(ps[p1][:n, :n], [(rqkT[:, 1, cc:cc + n], rqkT[:, 0, cc:cc + n])], [Rqk[c]], [Rps[p1]])
                    tt(S.dve, AT[b][:n, :n], ps[p1][:n, :n], DT[:n, hd, :n], ALU.mult, [Rps[p1], Rtab], [RAT[b]])
                    if c >= 1:
                        tt(S.pool, QfT[b][:, :n], rqkT[:, 0, cc:cc + n], qdf[:, hd, off:off + n], ALU.mult, [Rqk[c], Rtab], [RQf[b]])
                    if c < NCH - 1:
                        tt(S.pool, QbT[b][:, :n], rqkT[:, 0, cc:cc + n], qdb[:, hd, off:off + n], ALU.mult, [Rqk[c], Rtab], [RQb[b]])

                p3b = {}

                def s3b(c):
                    n, cc, b = ntok(c), col0(c), c % 2
                    pairs = [(AT[b][:n, :n], rvh[:n, c, :])]
                    rd = [RAT[b], Rrv[c]]
                    if c >= 1:
                        pairs.append((QfT[b][:, :n], sfb[c % 2][:, :]))
                        rd += [RQf[b], Rsfb[c % 2]]
                    if c < NCH - 1:
                        pairs.append((QbT[b][:, :n], sball[:, c, :]))
                        rd += [RQb[b], Rsb[c]]
                    p2 = psn()
                    mm(ps[p2][:n, :256], pairs, rd, [Rps[p2]])
                    if c < NCH - 1:
                        p4 = psn()
                        mm(ps[p4][:, :256], [(kdfh[:n, c, :], rvh[:n, c, :])], [Rkd[c], Rrv[c]], [Rps[p4]])
                        stt(S.dve, sf32[:], sf32[:], gch[:, hd:hd + 1], ps[p4][:, :256], ALU.mult, ALU.add,
                            [Rsf32, Rps[p4], Rtab], [Rsf32])
                        cp(S.dve, sfb[(c + 1) % 2][:, :], sf32[:], [Rsf32], [Rsfb[(c + 1) % 2]])
                    rms_stats(ps[p2][:n, :256], n, 256, sqo[:n, :], Rsqo, sso[:n, b:b + 1], rso[:n, b:b + 1], Rsso[b], Rrso[b], [Rps[p2]])
                    act(on[b][:n, :], ps[p2][:n, :256], AF.Copy, [Rps[p2], Rrso[b]], [Ron[b]], scale=rso[:n, b:b + 1])

                def s3c(c):
                    n, cc, b = ntok(c), col0(c), c % 2
                    p3 = psn()
                    for j in range(2):
                        tr(psb[p3][:, j * 128:j * 128 + n], on[b][:n, j * 128:(j + 1) * 128], n, [Ron[b]], [Rps[p3]], j == 1)
                    cp(S.dve, XT[:, 2 * hd:2 * hd + 2, cc:cc + n], psb[p3][:, 0:256].rearrange("p (j q) -> p j q", q=128)[:, :, :n],
                       [Rps[p3]], [RXT[c]])

                for i in range(NCH + 2):
                    if i < NCH:
                        s3a(i)
                    if 0 <= i - 1 < NCH:
                        s3b(i - 1)
                    if 0 <= i - 2 < NCH:
                        s3c(i - 2)
            S.barrier()
            M.top = m0

        def gate_rg(l, XT, RXT):
            m0 = M.top
            sg = [M.alloc(f"sgl{i}", [128, 512], F32) for i in range(2)]
            Rsg = [Res("sgl") for _ in range(2)]
            Rw_in = Rcast[("win", l)]
            it = 0
            for jb in range(2):
                wt, Rw = wtile()
                wload(wt, Rw, bwin[l], Rw_in, [(0, 2048 + jb * 512, 512)])
                for jj in range(4):
                    j = jb * 4 + jj
                    for (tc0, tn, chunks) in TILES:
                        pi = psn()
                        mm(ps[pi][:, :tn], [(wt[:, k, jj * 128:(jj + 1) * 128], hnT[:, k, tc0:tc0 + tn]) for k in range(8)],
                           hn_reads(chunks) + [Rw], [Rps[pi]])
                        b = it % 2
                        it += 1
                        act(sg[b][:, :tn], ps[pi][:, :tn], AF.Silu, [Rps[pi]], [Rsg[b]])
                        rx = [RXT[c] for c in chunks]
                        tt(S.pool, XT[:, j, tc0:tc0 + tn], XT[:, j, tc0:tc0 + tn], sg[b][:, :tn], ALU.mult, rx + [Rsg[b]], rx)
            S.barrier()
            M.top = m0

        def epilogue(l, srcT, Rsrc, bw, Rbw, gate_c0):
            m0 = M.top
            term = M.alloc("term", [128, 8, 528], BF16)
            Rterm = Res("term")
            sg = [M.alloc(f"sge{i}", [128, 512], F32) for i in range(2)]
            Rsg = [Res("sge") for _ in range(2)]
            Rw_in = Rcast[("win", l)]
            Rw_o = Rcast[("wo", l)]
            it = 0
            for pss in PASSES:
                tiles = [TILES[i] for i in pss]
                pc0 = tiles[0][0]
                for jb in range(4):
                    wa, Rwa = htile()
                    wload(wa, Rwa, bw[l], Rbw, [(0, jb * 256, 256)])
                    wg, Rwg = htile()
                    wload(wg, Rwg, bwin[l], Rw_in, [(0, gate_c0 + jb * 256, 256)])
                    for jj in range(2):
                        j = jb * 2 + jj
                        for (tc0, tn, chunks) in tiles:
                            pa = psn()
                            mm(ps[pa][:, :tn], [(wa[:, k, jj * 128:(jj + 1) * 128], srcT[:, k, tc0:tc0 + tn]) for k in range(8)],
                               [Rsrc[c] for c in chunks] + [Rwa], [Rps[pa]])
                            pg = psn()
                            mm(ps[pg][:, :tn], [(wg[:, k, jj * 128:(jj + 1) * 128], hnT[:, k, tc0:tc0 + tn]) for k in range(8)],
                               hn_reads(chunks) + [Rwg], [Rps[pg]])
                            b = it % 2
                            it += 1
                            act(sg[b][:, :tn], ps[pg][:, :tn], AF.Sigmoid, [Rps[pg]], [Rsg[b]])
                            tt(S.dve, term[:, j, tc0 - pc0:tc0 - pc0 + tn], ps[pa][:, :tn], sg[b][:, :tn], ALU.mult,
                               [Rps[pa], Rsg[b]], [Rterm])
                for cb in range(2):
                    wo2, Rwo2 = wtile()
                    wload(wo2, Rwo2, bwo[l], Rw_o, [(0, cb * 512, 512)])
                    for (tc0, tn, chunks) in tiles:
                        for c in chunks:
                            n, lc = ntok(c), col0(c) - pc0
                            po = psn()
                            mm(ps[po][:n, :512], [(term[:, k, lc:lc + n], wo2[:, k, :]) for k in range(8)], [Rterm, Rwo2], [Rps[po]])
                            tt(S.dve, h[:n, c, cb * 512:(cb + 1) * 512], h[:n, c, cb * 512:(cb + 1) * 512], ps[po][:n, :512], ALU.add,
                               [Rh[c], Rps[po]], [Rh[c]])
            S.barrier()
            M.top = m0

        def attention(l, AO, RAO):
            m0 = M.top
            akT = M.alloc("akT", [128, L], BF16)
            avg = M.alloc("avg", [128, NCH, 128], BF16)
            aqT = M.alloc("aqT", [128, 4, L], BF16)
            Rak = [Res(f"ak{c}") for c in range(NCH)]
            Rav = [Res(f"av{c}") for c in range(NCH)]
            Raq = [Res(f"aq{c}") for c in range(NCH)]
            t1 = [M.alloc("at1", [128, 4, 128], F32)]
            t2 = [M.alloc("at2", [128, 4, 128], F32)]
            qn = [M.alloc("aqn", [128, 4, 128], F32)]
            qb = [M.alloc(f"aqb{i}", [128, 4, 128], BF16) for i in range(2)]
            Rt1 = [Res("at1") for _ in range(2)]
            Rt2 = [[Res(f"at2{i}") for i in range(2)] for _ in range(2)]
            Rqn = [[Res(f"aqn{i}") for i in range(4)] for _ in range(2)]
            Rqb = [Res("aqb") for _ in range(2)]
            sqa = M.alloc("sqa", [128, 4, 128], BF16)
            Rsqa = [Res(f"sqa{i}") for i in range(4)]
            ssa = M.alloc("ssa", [128, 8], F32)
            rsa = M.alloc("rsa", [128, 8], F32)
            Rssa = [[Res(f"ssa{i}") for i in range(4)] for _ in range(2)]
            Rrsa = [Res("rsa") for _ in range(2)]
            NP = 4
            m_alias = M.top
            PT = [M.alloc(f"PT{i}", [128, 512], BF16) for i in range(NP)]
            RPT = [Res("PT") for _ in range(NP)]
            rec = [M.alloc(f"rec{i}", [128, 512], F32) for i in range(2)]
            Rrec = [Res("rec") for _ in range(2)]
            m_end = M.top
            M.top = m_alias
            t1.append(M.alloc("at1b", [128, 4, 128], F32))
            t2.append(M.alloc("at2b", [128, 4, 128], F32))
            qn.append(M.alloc("aqnb", [128, 4, 128], F32))
            M.top = max(M.top, m_end)
            Rw_in = Rcast[("win", l)]
            for g in range(2):
                wt, Rw = wtile()
                wload(wt, Rw, bwin[l], Rw_in, [(0, 4096 + g * 128, 128), (128, 4352 + g * 128, 128)])
                pjb, ptb, tbs = {}, {}, {}

                def ka(c):
                    tbs[c] = load_tb(c, "ax")
                    pjb[c] = proj_tok(c, wt, Rw, 256)

                def kb(c):
                    n, b = ntok(c), c % 2
                    tbt, Rt = tbs[c]
                    pi = pjb[c]
                    rms_stats(ps[pi][:n, 0:128], n, 128, sqa[:n, 0, :], Rsqa[0], ssa[:n, b:b + 1], rsa[:n, b:b + 1], Rssa[b][0], Rrsa[b], [Rps[pi]])
                    act(avg[:n, c, :], ps[pi][:n, 128:256], AF.Copy, [Rps[pi]], [Rav[c]])
                    stt(S.dve, qn[b][:n, 0, :], ps[pi][:n, 0:128], rsa[:n, b:b + 1], gk[:n, :], ALU.mult, ALU.mult,
                        [Rps[pi], Rrsa[b], Rg], [Rqn[b][0]])
                    rope(qn[b][:n, 0:1, :], n, 1, "ax", tbt, Rt, t1[b], t2[b], Rt1[b], Rt2[b], qb[b][:n, 0:1, :], Rqb[b], [Rqn[b][0]])
                    pt = psn()
                    ptb[c] = pt
                    tr(psb[pt][:, 0:n], qb[b][:n, 0, :], n, [Rqb[b]], [Rps[pt]], True)

                def kc_(c):
                    n, cc, pt = ntok(c), col0(c), ptb[c]
                    cp(S.dve, akT[:, cc:cc + n], psb[pt][:, 0:n], [Rps[pt]], [Rak[c]])

                for i in range(NCH + 3):
                    if i < NCH:
                        ka(i)
                    if 0 <= i - 2 < NCH:
                        kb(i - 2)
                    if 0 <= i - 3 < NCH:
                        kc_(i - 3)
                wt, Rw = wtile()
                wload(wt, Rw, bwin[l], Rw_in, [(0, 3072 + g * 512, 512)])
                pjb, ptb, tbs = {}, {}, {}

                def qa(c):
                    tbs[c] = load_tb(c, "ax")
                    pjb[c] = proj_tok(c, wt, Rw, 512)

                def qb_(c):
                    n, b = ntok(c), c % 2
                    tbt, Rt = tbs[c]
                    pi = pjb[c]
                    for hh in range(4):
                        act(sqa[:n, hh, :], ps[pi][:n, hh * 128:(hh + 1) * 128], AF.Square, [Rps[pi]], [Rsqa[hh], Rssa[b][hh]],
                            accum_out=ssa[:n, 4 * b + hh:4 * b + hh + 1])
                    act(rsa[:n, 4 * b:4 * b + 4], ssa[:n, 4 * b:4 * b + 4], AF.Ln, [Rssa[b], Rc], [Rrsa[b]], scale=1.0 / 128, bias=epst[:n, :])
                    act(rsa[:n, 4 * b:4 * b + 4], rsa[:n, 4 * b:4 * b + 4], AF.Exp, [Rrsa[b]], [Rrsa[b]], scale=-0.5)
                    for hh in range(4):
                        stt(S.dve, qn[b][:n, hh, :], ps[pi][:n, hh * 128:(hh + 1) * 128], rsa[:n, 4 * b + hh:4 * b + hh + 1], gq[:n, :],
                            ALU.mult, ALU.mult, [Rps[pi], Rrsa[b], Rg], [Rqn[b][hh]])
                    rope(qn[b][:n, :, :], n, 4, "ax", tbt, Rt, t1[b], t2[b], Rt1[b], Rt2[b], qb[b][:n, :, :], Rqb[b], [Rqn[b]])
                    pt = psn()
                    ptb[c] = pt
                    for hh in range(4):
                        tr(psb[pt][:, hh * 128:hh * 128 + n], qb[b][:n, hh, :], n, [Rqb[b]], [Rps[pt]], hh == 3)

                def qc_(c):
                    n, cc, pt = ntok(c), col0(c), ptb[c]
                    cp(S.dve, aqT[:, :, cc:cc + n], psb[pt][:, 0:512].rearrange("p (j q) -> p j q", q=128)[:, :, :n],
                       [Rps[pt]], [Raq[c]])

                for i in range(NCH + 3):
                    if i < NCH:
                        qa(i)
                    if 0 <= i - 2 < NCH:
                        qb_(i - 2)
                    if 0 <= i - 3 < NCH:
                        qc_(i - 3)
                S.barrier()
                items = []
                unit = 0
                for kc in range(NCH):
                    items.append((0, None, kc, unit))
                for ti in range(1, len(TILES)):
                    for hh in range(4):
                        unit += 1
                        for kc in range(NCH):
                            items.append((ti, hh, kc, unit))
                LA = 2
                slot = {}

                def sfront(i):
                    ti, hh, kc, un = items[i]
                    tc0, tn, chunks = TILES[ti]
                    nk, kc0 = ntok(kc), col0(kc)
                    pS = psn(4)
                    rq = [Raq[c] for c in chunks] + [Rak[kc]]
                    if hh is None:
                        for h4 in range(4):
                            S.op(S.pe, lambda e, h4=h4: e.matmul(ps[pS][:nk, h4 * tn:(h4 + 1) * tn], lhsT=akT[:, kc0:kc0 + nk],
                                                               rhs=aqT[:, h4, tc0:tc0 + tn], start=True, stop=True),
                                 reads=rq, writes=[Rps[pS]], signal=(h4 == 3))
                        w = 4 * tn
                    else:
                        mm(ps[pS][:nk, :tn], [(akT[:, kc0:kc0 + nk], aqT[:, hh, tc0:tc0 + tn])], rq, [Rps[pS]])
                        w = tn
                    pb = i % NP
                    slot[i] = pb
                    act(PT[pb][:nk, :w], ps[pS][:nk, :w], AF.Exp, [Rps[pS]], [RPT[pb]], scale=QK_SCALE)

                def sback(i):
                    ti, hh, kc, un = items[i]
                    tc0, tn, chunks = TILES[ti]
                    nk = ntok(kc)
                    w = 4 * tn if hh is None else tn
                    po, pm = 4 + un % 2, 6 + un % 2
                    pb = slot.pop(i)
                    S.op(S.pe, lambda e: e.matmul(ps[po][:, :w], lhsT=avg[:nk, kc, :], rhs=PT[pb][:nk, :w],
                                                  start=(kc == 0), stop=(kc == NCH - 1)),
                         reads=[Rav[kc], RPT[pb]], writes=[Rps[po]], signal=False)
                    S.op(S.pe, lambda e: e.matmul(ps[pm][:, :w], lhsT=ones[:nk, :], rhs=PT[pb][:nk, :w],
                                                  start=(kc == 0), stop=(kc == NCH - 1)),
                         reads=[Rc, RPT[pb]], writes=[Rps[pm]], signal=True)
                    if kc == NCH - 1:
                        rb = un % 2
                        S.op(S.dve, lambda e: e.reciprocal(out=rec[rb][:, :w], in_=ps[pm][:, :w]),
                             reads=[Rps[pm]], writes=[Rrec[rb]])
                        if hh is None:
                            tt(S.dve, AO[:, g * 4:(g + 1) * 4, tc0:tc0 + tn],
                               ps[po][:, :w].rearrange("p (h q) -> p h q", h=4), rec[rb][:, :w].rearrange("p (h q) -> p h q", h=4),
                               ALU.mult, [Rps[po], Rrec[rb]], [RAO[c] for c in chunks])
                        else:
                            tt(S.dve, AO[:, g * 4 + hh, tc0:tc0 + tn], ps[po][:, :tn], rec[rb][:, :tn], ALU.mult,
                               [Rps[po], Rrec[rb]], [RAO[c] for c in chunks])

                for i in range(len(items) + LA):
                    if i < len(items):
                        sfront(i)
                    if i >= LA:
                        sback(i - LA)
                if g == 0:
                    S.barrier()
            S.barrier()
            M.top = m0

        def ffn(l):
            m0 = M.top
            gT = M.alloc("gT", [128, 22, 1040], BF16)
            RgT = Res("gT")
            wfo = M.alloc("wfo", [128, 22, 512], BF16)
            Rwfo = Res("wfo")
            sa = [M.alloc(f"sa{i}", [128, 512], F32) for i in range(2)]
            Rsa = [Res("sa") for _ in range(2)]
            Rw_fi = Rcast[("wfi", l)]
            Rw_fo = Rcast[("wfo", l)]
            it = 0
            def load_wfo(cb):
                S.dma(S.sp, S.rot_chan("w", 4), wfo[:], bwfo[l][:, cb * 512:(cb + 1) * 512].rearrange("(k p) c -> p k c", p=128),
                      reads=Rw_fo, writes=[Rwfo])

            for pss in FFN_PASSES:
                tiles = [TILES[i] for i in pss]
                pc0 = tiles[0][0]
                load_wfo(0)
                for jb in range(11):
                    wa, Rwa = htile()
                    wload(wa, Rwa, bwfi[l], Rw_fi, [(0, jb * 256, 256)])
                    wu, Rwu = htile()
                    wload(wu, Rwu, bwfi[l], Rw_fi, [(0, DFF + jb * 256, 256)])
                    for jj in range(2):
                        j = jb * 2 + jj
                        for (tc0, tn, chunks) in tiles:
                            pa = psn()
                            mm(ps[pa][:, :tn], [(wa[:, k, jj * 128:(jj + 1) * 128], hnT[:, k, tc0:tc0 + tn]) for k in range(8)],
                               hn_reads(chunks) + [Rwa], [Rps[pa]])
                            pu = psn()
                            mm(ps[pu][:, :tn], [(wu[:, k, jj * 128:(jj + 1) * 128], hnT[:, k, tc0:tc0 + tn]) for k in range(8)],
                               hn_reads(chunks) + [Rwu], [Rps[pu]])
                            b = it % 2
                            it += 1
                            act(sa[b][:, :tn], ps[pa][:, :tn], AF.Silu, [Rps[pa]], [Rsa[b]])
                            tt(S.dve, gT[:, j, tc0 - pc0:tc0 - pc0 + tn], ps[pu][:, :tn], sa[b][:, :tn], ALU.mult,
                               [Rps[pu], Rsa[b]], [RgT])
                for cb in range(2):
                    if cb == 1:
                        load_wfo(1)
                    for (tc0, tn, chunks) in tiles:
                        for c in chunks:
                            n, lc = ntok(c), col0(c) - pc0
                            po = psn()
                            mm(ps[po][:n, :512], [(gT[:, k, lc:lc + n], wfo[:, k, :]) for k in range(22)], [RgT, Rwfo], [Rps[po]])
                            tt(S.dve, h[:n, c, cb * 512:(cb + 1) * 512], h[:n, c, cb * 512:(cb + 1) * 512], ps[po][:n, :512], ALU.add,
                               [Rh[c], Rps[po]], [Rh[c]])
            S.barrier()
            M.top = m0

        def final_phase(s):
            m0 = M.top
            gbf = M.alloc("gbf", [128, D], F32)
            Rgbf = Res("gbf")
            yo = [M.alloc(f"yo{i}", [128, D], F32) for i in range(2)]
            Ryo = [Res("yo") for _ in range(2)]
            sqj = M.alloc("sqjf", [128, D], BF16)
            Rsqj = Res("sqjf")
            S.dma(S.sp, S.chan("g"), gbf[:], nfin_d.partition_broadcast(128), writes=[Rgbf])
            if s + 1 < NS:
                load_meta()
            for c in range(1, NCH):
                b = c % 2
                rms_stats(h[:, c, :], 128, D, sqj[:, :], Rsqj, ssn[:, c:c + 1], rsn[:, c:c + 1], Rssn[c], Rrsn[c], [Rh[c]])
                stt(S.dve, yo[b][:], h[:, c, :], rsn[:, c:c + 1], gbf[:], ALU.mult, ALU.mult, [Rh[c], Rrsn[c], Rgbf], [Ryo[b]])
                S.dma(S.sp, S.rot_chan("y", 2), y_d[s, (c - 1) * 128:c * 128, :], yo[b][:], reads=[Ryo[b]])
                if c % 4 == 0 and s + 1 < NS:
                    load_x(s + 1, c // 4 - 1)
            S.barrier()
            M.top = m0

        RXT = [Res(f"XT{c}") for c in range(NCH)]
        def load_meta():
            S.dma(S.sp, S.rot_chan("x", 2), h[:NMETA, 0, :], meta_d, writes=[Rh[0]])

        def load_x(sq, t):
            S.dma(S.sp, S.rot_chan("x", 2), h[:, 1 + 4 * t:5 + 4 * t, :],
                  x_d[sq, 512 * t:512 * (t + 1), :].rearrange("(c p) d -> p c d", p=128),
                  writes=[Rh[1 + 4 * t + i] for i in range(4)])

        for s in range(NS):
            if s == 0:
                load_meta()
                for t in range(4):
                    load_x(0, t)
            for l in range(DEPTH):
                norm_phase(nmix_d[l])
                dec_tables(l)
                m_mix = M.top
                XT = M.alloc("XT", [128, 8, L], BF16)
                retention(l, XT, RXT)
                gate_rg(l, XT, RXT)
                epilogue(l, XT, RXT, bwro, Rcast[("wro", l)], 4608)
                attention(l, XT, RXT)
                epilogue(l, XT, RXT, bwao, Rcast[("wao", l)], 5632)
                M.top = m_mix
                norm_phase(nffn_d[l])
                ffn(l)
            final_phase(s)
        S.wait_all(S.sp)
        S.emit()
    return nc


_TB_CACHE = {}


def rope_tables():
    if "tb" in _TB_CACHE:
        return _TB_CACHE["tb"]
    f32 = np.float32
    ret_inv = (f32(10000.0) ** (-np.linspace(0.0, 1.0, 64, dtype=f32))).astype(f32)
    ret_ang = (np.arange(L, dtype=f32)[:, None] * ret_inv[None, :]).astype(f32)
    rows = SEQ // 64
    zeros = np.zeros((NMETA,), f32)
    row_ids = np.concatenate([zeros, np.repeat(np.arange(rows, dtype=f32), 64)])
    col_ids = np.concatenate([zeros, np.tile(np.arange(64, dtype=f32), rows)])
    ax_inv = (f32(10000.0) ** (-np.arange(32, dtype=f32) * f32(2.0) / f32(64))).astype(f32)
    row_ang = (row_ids[:, None] * ax_inv[None, :]).astype(f32)
    col_ang = (col_ids[:, None] * ax_inv[None, :]).astype(f32)
    tb = np.zeros((L, 4, 128), f32)
    c, s = np.cos(ret_ang), np.sin(ret_ang)
    tb[:, 0, :64], tb[:, 0, 64:] = c, c
    tb[:, 1, :64], tb[:, 1, 64:] = -s, s
    cr, sr, cc, sc = np.cos(row_ang), np.sin(row_ang), np.cos(col_ang), np.sin(col_ang)
    tb[:, 2, 0:32], tb[:, 2, 32:64], tb[:, 2, 64:96], tb[:, 2, 96:128] = cr, cr, cc, cc
    tb[:, 3, 0:32], tb[:, 3, 32:64], tb[:, 3, 64:96], tb[:, 3, 96:128] = -sr, sr, -sc, sc
    _TB_CACHE["tb"] = tb
    return tb


def make_in_maps(xs_per_core, meta_tokens, norm_mix, w_in, ret_decay, q_norm, k_norm, w_ret_o, w_att_o, w_out,
                 norm_ffn, w_ffn_in, w_ffn_out, norm_final):
    c = lambda a: np.ascontiguousarray(np.asarray(a, dtype=np.float32))
    depth = np.asarray(w_in).shape[0]
    shared = {
        "meta": c(meta_tokens), "norm_mix": c(norm_mix), "w_in": c(w_in),
        "ret_decay": c(np.asarray(ret_decay).reshape(depth, 8)), "q_norm": c(q_norm), "k_norm": c(k_norm),
        "w_ret_o": c(w_ret_o), "w_att_o": c(w_att_o), "w_out": c(w_out), "norm_ffn": c(norm_ffn),
        "w_ffn_in": c(w_ffn_in), "w_ffn_out": c(w_ffn_out), "norm_final": c(norm_final), "rope_tb": rope_tables(),
    }
    return [dict(shared, x=c(xc)) for xc in xs_per_core]


def kernel(x_prompt, x_sample, meta_tokens, norm_mix, w_in, ret_decay, q_norm, k_norm, w_ret_o, w_att_o, w_out,
           norm_ffn, w_ffn_in, w_ffn_out, norm_final):
    xp = np.asarray(x_prompt, dtype=np.float32)
    xs = np.asarray(x_sample, dtype=np.float32)
    allx = np.concatenate([xp, xs], axis=0)
    nseq = allx.shape[0]
    per = nseq // N_CORES
    depth = np.asarray(w_in).shape[0]
    nc = bass.Bass("TRN2", target_bir_lowering=False)
    build_program(nc, per, depth)
    in_maps = make_in_maps([allx[i * per:(i + 1) * per] for i in range(N_CORES)], meta_tokens, norm_mix, w_in, ret_decay,
                           q_norm, k_norm, w_ret_o, w_att_o, w_out, norm_ffn, w_ffn_in, w_ffn_out, norm_final)
    res = run_bass_kernel_spmd(nc, in_maps, core_ids=list(range(N_CORES)))
    y = np.concatenate([np.asarray(r["y"], dtype=np.float32) for r in res.results], axis=0)
    return (y[:xp.shape[0]], y[xp.shape[0]:])
```
